# Optimizing a Trainium2 kernel written in Bass

```python
import jax, jax.numpy as jnp
from jax import lax
import numpy as np

D_MODEL = 1024
BATCH = 4
SEQ = 4096
DEPTH = 1

N_META = 16
N_Q_HEADS = 16
N_KV_HEADS = 2
HEAD_DIM = 64
GROUP = N_Q_HEADS // N_KV_HEADS
ROT_DIM = HEAD_DIM // 4
ROPE_THETA = 500000.0
WINDOW = 128
BLOCK = 128
ATTN_WIDTH = N_Q_HEADS * HEAD_DIM
KV_WIDTH = N_KV_HEADS * HEAD_DIM
CONV_CH = D_MODEL
CONV_K = 31
FFN_DIM = 2816
FFN_CONV_K = 3
IN_WIDTH = ATTN_WIDTH + 2 * KV_WIDTH + 2 * CONV_CH + 2 * D_MODEL
RMS_EPS = 1e-6
LN_EPS = 1e-5
NEG_INF = -1e30

kernel_name = "hybrid_swa_sink_conformer_convffn_block"


def rms_norm(x, g):
    xf = x.astype(jnp.float32)
    y = xf * lax.rsqrt(jnp.mean(xf * xf, axis=-1, keepdims=True) + RMS_EPS)
    return (y * g.astype(jnp.float32)).astype(x.dtype)


def layer_norm(x, g, b):
    xf = x.astype(jnp.float32)
    mu = jnp.mean(xf, axis=-1, keepdims=True)
    var = jnp.mean(jnp.square(xf - mu), axis=-1, keepdims=True)
    y = (xf - mu) * lax.rsqrt(var + LN_EPS)
    return (y * g.astype(jnp.float32) + b.astype(jnp.float32)).astype(x.dtype)


def causal_dwconv(x, w, b):
    k = w.shape[0]
    y = lax.conv_general_dilated(
        x, w[:, None, :].astype(x.dtype), window_strides=(1,), padding=[(k - 1, 0)],
        dimension_numbers=("NWC", "WIO", "NWC"), feature_group_count=x.shape[-1])
    return y + b.astype(x.dtype)


def partial_rope(x, pos):
    half = ROT_DIM // 2
    inv_freq = ROPE_THETA ** (-jnp.arange(half, dtype=jnp.float32) * 2.0 / ROT_DIM)
    ang = pos.astype(jnp.float32)[:, None] * inv_freq[None, :]
    cos = jnp.cos(ang)[None, :, None, :]
    sin = jnp.sin(ang)[None, :, None, :]
    xr = x[..., :ROT_DIM].astype(jnp.float32)
    x1, x2 = xr[..., :half], xr[..., half:]
    rot = jnp.concatenate([x1 * cos - x2 * sin, x2 * cos + x1 * sin], axis=-1).astype(x.dtype)
    return jnp.concatenate([rot, x[..., ROT_DIM:]], axis=-1)


def sliding_window_sink_attention(q, k, v, sinks):
    bsz, seq_len = q.shape[0], q.shape[1]
    pad = BLOCK - N_META
    padded = seq_len + pad
    nb = padded // BLOCK
    scale = HEAD_DIM ** -0.5

    def pad_front(a):
        return jnp.pad(a, ((0, 0), (pad, 0), (0, 0), (0, 0)))

    def shift_block(a):
        return jnp.concatenate([jnp.zeros_like(a[:, :1]), a[:, :-1]], axis=1)

    qb = (pad_front(q) * scale).reshape(bsz, nb, BLOCK, N_KV_HEADS, GROUP, HEAD_DIM)
    kb = pad_front(k).reshape(bsz, nb, BLOCK, N_KV_HEADS, HEAD_DIM)
    vb = pad_front(v).reshape(bsz, nb, BLOCK, N_KV_HEADS, HEAD_DIM)
    k_meta = jnp.broadcast_to(k[:, None, :N_META], (bsz, nb, N_META, N_KV_HEADS, HEAD_DIM))
    v_meta = jnp.broadcast_to(v[:, None, :N_META], (bsz, nb, N_META, N_KV_HEADS, HEAD_DIM))
    keys = jnp.concatenate([k_meta, shift_block(kb), kb], axis=2)
    vals = jnp.concatenate([v_meta, shift_block(vb), vb], axis=2)

    tpos = (jnp.arange(padded) - pad).reshape(nb, BLOCK)
    tq = tpos[:, :, None]
    t_meta = jnp.arange(N_META)[None, None, :]
    t_loc = jnp.concatenate([tpos - BLOCK, tpos], axis=1)[:, None, :]
    meta_ok = jnp.broadcast_to(t_meta <= tq, (nb, BLOCK, N_META))
    loc_ok = (t_loc >= N_META) & (t_loc <= tq) & (tq - t_loc < WINDOW)
    mask = jnp.concatenate([meta_ok, loc_ok], axis=-1)

    s = jnp.einsum("bnqhgd,bnkhd->bnhgqk", qb, keys).astype(jnp.float32)
    s = jnp.where(mask[None, :, None, None], s, NEG_INF)
    sink = sinks.astype(jnp.float32).reshape(N_KV_HEADS, GROUP)[None, None, :, :, None, None]
    sink = jnp.broadcast_to(sink, s.shape[:-1] + (1,))
    probs = jax.nn.softmax(jnp.concatenate([s, sink], axis=-1), axis=-1)[..., :-1]
    o = jnp.einsum("bnhgqk,bnkhd->bnqhgd", probs.astype(v.dtype), vals)
    return o.reshape(bsz, padded, ATTN_WIDTH)[:, pad:]


def hybrid_mixer(h, pos, w_in, b_in, attn_sinks, w_attn_proj, conv_dw_w, conv_dw_b,
                 conv_ln_g, conv_ln_b, w_conv_proj, b_conv_proj, w_out):
    bsz, seq_len = h.shape[0], h.shape[1]
    proj = h @ w_in + b_in
    cuts = np.cumsum([ATTN_WIDTH, KV_WIDTH, KV_WIDTH, 2 * CONV_CH, D_MODEL]).tolist()
    q, k, v, glu_in, gate_a, gate_c = jnp.split(proj, cuts, axis=-1)

    q = partial_rope(q.reshape(bsz, seq_len, N_Q_HEADS, HEAD_DIM), pos)
    k = partial_rope(k.reshape(bsz, seq_len, N_KV_HEADS, HEAD_DIM), pos)
    v = v.reshape(bsz, seq_len, N_KV_HEADS, HEAD_DIM)
    attn = sliding_window_sink_attention(q, k, v, attn_sinks) @ w_attn_proj

    a, g = jnp.split(glu_in, 2, axis=-1)
    c = causal_dwconv(a * jax.nn.sigmoid(g), conv_dw_w, conv_dw_b)
    c = jax.nn.silu(layer_norm(c, conv_ln_g, conv_ln_b))
    conv = c @ w_conv_proj + b_conv_proj

    merged = jax.nn.sigmoid(gate_a) * attn + jax.nn.sigmoid(gate_c) * conv
    return merged @ w_out


def conv_ffn(h, w_up, ffn_dw_w, ffn_dw_b, w_down):
    u = causal_dwconv(h @ w_up, ffn_dw_w, ffn_dw_b)
    gate, val = jnp.split(u, 2, axis=-1)
    return (jax.nn.silu(gate) * val) @ w_down


def setup_inputs(seed: int = 0) -> dict:
    key = jax.random.key(seed)
    ks = jax.random.split(key, 24)
    f32 = jnp.float32

    def nrm(k, shape, scale):
        return jax.random.normal(k, shape, f32) * scale

    def gain(k, shape):
        return 1.0 + 0.1 * jax.random.normal(k, shape, f32)

    L = DEPTH
    return {
        "x": nrm(ks[0], (BATCH, SEQ, D_MODEL), 1.0),
        "meta_tokens": nrm(ks[1], (N_META, D_MODEL), 1.0),
        "norm_pre_mix": gain(ks[2], (L, D_MODEL)),
        "norm_post_mix": gain(ks[3], (L, D_MODEL)),
        "w_in": nrm(ks[4], (L, D_MODEL, IN_WIDTH), D_MODEL ** -0.5),
        "b_in": nrm(ks[5], (L, IN_WIDTH), 0.02),
        "attn_sinks": nrm(ks[6], (L, N_Q_HEADS), 0.5),
        "w_attn_proj": nrm(ks[7], (L, ATTN_WIDTH, D_MODEL), ATTN_WIDTH ** -0.5),
        "conv_dw_w": nrm(ks[8], (L, CONV_K, CONV_CH), CONV_K ** -0.5),
        "conv_dw_b": nrm(ks[9], (L, CONV_CH), 0.02),
        "conv_ln_g": gain(ks[10], (L, CONV_CH)),
        "conv_ln_b": nrm(ks[11], (L, CONV_CH), 0.02),
        "w_conv_proj": nrm(ks[12], (L, CONV_CH, D_MODEL), CONV_CH ** -0.5),
        "b_conv_proj": nrm(ks[13], (L, D_MODEL), 0.02),
        "w_out": nrm(ks[14], (L, D_MODEL, D_MODEL), D_MODEL ** -0.5),
        "norm_pre_ffn": gain(ks[15], (L, D_MODEL)),
        "norm_post_ffn": gain(ks[16], (L, D_MODEL)),
        "w_up": nrm(ks[17], (L, D_MODEL, 2 * FFN_DIM), D_MODEL ** -0.5),
        "ffn_dw_w": nrm(ks[18], (L, FFN_CONV_K, 2 * FFN_DIM), FFN_CONV_K ** -0.5),
        "ffn_dw_b": nrm(ks[19], (L, 2 * FFN_DIM), 0.02),
        "w_down": nrm(ks[20], (L, FFN_DIM, D_MODEL), FFN_DIM ** -0.5),
    }


def reference(x, meta_tokens, norm_pre_mix, norm_post_mix, w_in, b_in, attn_sinks, w_attn_proj,
              conv_dw_w, conv_dw_b, conv_ln_g, conv_ln_b, w_conv_proj, b_conv_proj, w_out,
              norm_pre_ffn, norm_post_ffn, w_up, ffn_dw_w, ffn_dw_b, w_down):
    bsz = x.shape[0]
    meta = jnp.broadcast_to(meta_tokens.astype(x.dtype)[None], (bsz, N_META, D_MODEL))
    h = jnp.concatenate([meta, x], axis=1)
    pos = jnp.arange(h.shape[1])
    for l in range(DEPTH):
        mix = hybrid_mixer(rms_norm(h, norm_pre_mix[l]), pos, w_in[l], b_in[l], attn_sinks[l],
                           w_attn_proj[l], conv_dw_w[l], conv_dw_b[l], conv_ln_g[l], conv_ln_b[l],
                           w_conv_proj[l], b_conv_proj[l], w_out[l])
        h = h + rms_norm(mix, norm_post_mix[l])
        ffn = conv_ffn(rms_norm(h, norm_pre_ffn[l]), w_up[l], ffn_dw_w[l], ffn_dw_b[l], w_down[l])
        h = h + rms_norm(ffn, norm_post_ffn[l])
    return h[:, N_META:]
```

```python
import numpy as np
import concourse.bass as bass
import concourse.mybir as mybir
from concourse.bass_utils import run_bass_kernel_spmd

dt = mybir.dt
F32 = dt.float32
BF16 = dt.bfloat16
AF = mybir.ActivationFunctionType
ALU = mybir.AluOpType
AX = mybir.AxisListType

D = 1024
NMETA = 16
HD = 64
FFN = 2816
NF = FFN // 128
CONVK = 31
IN_W = 5376
CELL = 256
ESZ = {F32: 4, BF16: 2}


def _esz(d):
    if d == F32:
        return 4
    if d == BF16:
        return 2
    raise ValueError(d)


class Op:
    __slots__ = ("eng", "fn", "idx", "deps", "dma", "val", "inc", "cnt", "big", "waits")


class Prog:
    ENG = ("pe", "act", "dve", "pool", "sp")

    def __init__(self, nc):
        self.nc = nc
        self.ops = []
        self.lastw = {}
        self.readers = {}
        self.streams = {}
        self.always_sync_same = False

    def cells(self, ap):
        t = ap.tensor
        cls = type(t).__name__
        if "DRam" in cls or "DRAM" in cls:
            return ()
        name = t.name
        is_psum = "PSum" in cls or "Psum" in cls or "PSUM" in cls
        if is_psum:
            self._psum = True
        esz = _esz(ap.dtype)
        a = [list(x) for x in ap.ap]
        ps = a[0][0]
        off = ap.offset % ps if ps else ap.offset
        dims = a[1:]
        if not dims:
            dims = [[1, 1]]
        ls, ln = dims[-1]
        span = (ln - 1) * abs(ls) + 1
        starts = [off]
        for s, n in dims[:-1]:
            starts = [b + i * s for b in starts for i in range(n)]
        out = set()
        CELL = 8 if name == "small" else (2048 if is_psum else 256)
        for b in starts:
            lo = (b * esz) // CELL
            hi = ((b + span) * esz - 1) // CELL
            for c in range(lo, hi + 1):
                out.add((name, c))
        return out

    def add(self, eng, fn, reads=(), writes=(), big=False, dma=None, extra_r=(), extra_w=()):
        op = Op()
        op.eng, op.fn, op.idx, op.big, op.dma = eng, fn, len(self.ops), big, dma
        op.inc = False
        op.cnt = 0
        deps = set()
        rc = set(extra_r)
        wc = set(extra_w)
        for ap in reads:
            self._psum = False
            cs = set(self.cells(ap))
            if self._psum:
                wc |= cs
            else:
                rc |= cs
        for ap in writes:
            wc |= set(self.cells(ap))
        for c in rc:
            w = self.lastw.get(c)
            if w is not None:
                deps.add(w)
        for c in wc:
            w = self.lastw.get(c)
            if w is not None:
                deps.add(w)
            for r in self.readers.get(c, ()):
                deps.add(r)
        deps.discard(op.idx)
        op.deps = deps
        for c in wc:
            self.lastw[c] = op.idx
            self.readers[c] = []
        for c in rc:
            if c not in wc:
                self.readers.setdefault(c, []).append(op.idx)
        if dma is not None:
            st = self.streams.setdefault(dma, [None, 0])
            st[1] += 16
            op.val = st[1]
        self.ops.append(op)
        return op

    def finalize_and_emit(self, final_streams=()):
        nc = self.nc
        ops = self.ops
        for op in ops:
            for d in op.deps:
                dop = ops[d]
                if dop.dma is None:
                    if dop.eng == op.eng and op.dma is None:
                        if dop.eng == "pe":
                            continue
                        if dop.big and not self.always_sync_same:
                            continue
                    dop.inc = True
        cnt = {e: 0 for e in self.ENG}
        for op in ops:
            if op.dma is None and op.inc:
                cnt[op.eng] += 1
                op.cnt = cnt[op.eng]
        self.sems = {e: nc.alloc_semaphore("s_" + e) for e in self.ENG}
        for name, st in self.streams.items():
            st[0] = nc.alloc_semaphore("d_" + name)
        known = {e: {} for e in self.ENG}
        for op in ops:
            need = {}
            for d in op.deps:
                dop = ops[d]
                if dop.dma is not None:
                    key = ("d", dop.dma)
                    v = dop.val
                else:
                    if dop.eng == op.eng and op.dma is None:
                        if dop.eng == "pe":
                            continue
                        if dop.big and not self.always_sync_same:
                            continue
                    key = ("e", dop.eng)
                    v = dop.cnt
                if need.get(key, 0) < v:
                    need[key] = v
            w = []
            kn = known[op.eng]
            for key, v in need.items():
                if kn.get(key, 0) >= v:
                    continue
                kn[key] = v
                sem = self.streams[key[1]][0] if key[0] == "d" else self.sems[key[1]]
                w.append((sem, v))
            op.waits = w
        by = {e: [o for o in ops if o.eng == e] for e in self.ENG}
        handles = {"pe": "tensor", "act": "scalar", "dve": "vector", "pool": "gpsimd", "sp": "sync"}
        stats = {e: len(by[e]) for e in self.ENG}
        self.stats = stats

        def emit(ename, e):
            for op in by[ename]:
                for sem, v in op.waits:
                    e.wait_ge(sem, v)
                ins = op.fn(e)
                if op.dma is not None:
                    ins.then_inc(self.streams[op.dma][0], 16)
                elif op.inc:
                    ins.then_inc(self.sems[ename], 1)
            if ename == "sp":
                for s in final_streams:
                    st = self.streams[s]
                    e.wait_ge(st[0], st[1])

        with nc.Block() as block:
            @block.tensor
            def _(e):
                emit("pe", e)

            @block.scalar
            def _(e):
                emit("act", e)

            @block.vector
            def _(e):
                emit("dve", e)

            @block.gpsimd
            def _(e):
                emit("pool", e)

            @block.sync
            def _(e):
                emit("sp", e)


def tiles(lo, hi, w=512):
    out = []
    t = lo
    while t < hi:
        out.append((t, min(t + w, hi)))
        t += w
    return out


def build(NOWN=16, dbg=False):
    NB = NOWN + 2
    NT = NB * 128
    NTM = NT + NMETA
    M0 = 128
    MW = NT - M0
    OWN0 = 256
    NHALF = 2 if NOWN >= 8 else 1
    HTOK = NOWN * 128 // NHALF

    nc = bass.Bass("TRN2", target_bir_lowering=False)
    P = Prog(nc)

    def din(name, shape, d=F32):
        return nc.dram_tensor(name, list(shape), d, kind="ExternalInput").ap()

    xin = din("xin", [NTM, D])
    win_l = din("win_l", [43, 128, 8, 128])
    wap_l = din("wap_l", [8, 128, 8, 128])
    wcp_l = din("wcp_l", [8, 128, 8, 128])
    wout_l = din("wout_l", [128, 8, D])
    wup_l = din("wup_l", [44, 128, 8, 128])
    wdn_l = din("wdn_l", [128, NF, D])
    ropeC = din("ropeC", [128, NTM])
    ropeS = din("ropeS", [128, NTM])
    cst_f = din("cst_f", [128, 576])
    cst_b = din("cst_b", [128, 1152])
    g4 = din("g4", [4, 128, D])
    bv_bc = din("bv_bc", [128, 128])
    y = nc.dram_tensor("y", [NOWN * 128, D], F32, kind="ExternalOutput").ap()
    h1s = nc.dram_tensor("h1s", [NOWN * 128, D], F32).ap()

    XW = ((NTM + 127) // 128) * 128
    SZ_A = 8 * XW * 2
    SZ_B = 8 * MW * 2
    o_A = 0
    o_C = o_A + SZ_A
    o_B = o_C + SZ_B
    o_D = o_B + SZ_B
    o_W = o_D + SZ_B
    NSLOT = 6
    o_KV = o_W + NSLOT * 2048
    SZ_K = 2 * NTM * 2
    SZ_V = (NB + 1) * 130 * 2
    o_X = ((o_KV + SZ_K + SZ_V + 255) // 256) * 256
    o_T = o_X + 4 * 4096
    SZ_T = 20992
    o_S = o_T + SZ_T
    SZ_S = 576 * 4 + 1152 * 2
    TOTAL = o_S + SZ_S
    SMALL = NOWN < 16
    if SMALL:
        o_E = TOTAL
        TOTAL += NF * D * 2 + 8 * D * 2 + NF * HTOK * 2
    assert TOTAL <= 212040, TOTAL
    arena = nc.alloc_sbuf_tensor("arena", [128, TOTAL // 2], BF16)

    def bf(off, n):
        assert off % 2 == 0
        return arena[:, off // 2: off // 2 + n]

    def f32(off, n):
        assert off % 4 == 0
        return arena[:, off // 2: off // 2 + 2 * n].bitcast(F32)

    def v3(ap, a):
        return ap.rearrange("p (a b) -> p a b", a=a)

    xT = v3(bf(o_A, 8 * XW), 8)
    Bq = v3(bf(o_B, 8 * MW), 8)
    Cg = v3(bf(o_C, 8 * MW), 8)
    Dc = v3(bf(o_D, 8 * MW), 8)
    kT = v3(bf(o_KV, 2 * NTM), 2)
    vA = bf(o_KV + SZ_K, (NB + 1) * 130).rearrange("p (b g e) -> p b g e", b=NB + 1, g=2)
    ring = [v3(bf(o_W + i * 2048, 1024), 8) for i in range(NSLOT)]
    xs = [f32(o_X + i * 4096, 1024) for i in range(2)]
    gt = [f32(o_X + 8192 + i * 4096, 1024) for i in range(2)]
    cf = f32(o_S, 576)
    cb = bf(o_S + 2304, 1152)
    small = nc.alloc_sbuf_tensor("small", [128, 192], F32)[:, :]
    c_bin = cf[:, 0:43]
    c_cw = cf[:, 64:64 + 8 * 31].rearrange("p (c k) -> p c k", c=8)
    c_cb = cf[:, 320:328]
    c_lg = cf[:, 328:336]
    c_lb = cf[:, 336:344]
    c_bcp = cf[:, 344:352]
    c_fw = cf[:, 352:352 + 132].rearrange("p (c k) -> p c k", c=44)
    c_fb = cf[:, 484:528]
    c_sink = cf[:, 528:544]
    ident = cb[:, 0:128]
    ones = cb[:, 128:256]
    m_same = cb[:, 256:384]
    m_prev = cb[:, 384:512]
    m1_same = cb[:, 512:640]
    m1_prev = cb[:, 640:768]
    m2_prev = cb[:, 768:896]
    m1_meta = cb[:, 896:1024]
    padmask = cb[:, 1024:1152]

    pst = nc.alloc_psum_tensor("ps", [128, 4096], F32)
    psb = [pst[:, i * 512:(i + 1) * 512] for i in range(8)]
    ps_i = {"all": 0, "att": 0, "ffn": 0}

    def psum2():
        i = ps_i["all"]
        if i % 2:
            i = (i + 1) % 8
        ps_i["all"] = (i + 2) % 8
        return pst[:, i * 512:(i + 2) * 512]

    def psum(pool="all"):
        if pool == "att":
            i = 3 + ps_i["att"]
            ps_i["att"] = (ps_i["att"] + 1) % 3
            return psb[i]
        if pool == "ln":
            return psum("att")
        if pool == "ffn":
            i = ps_i["ffn"]
            ps_i["ffn"] = (i + 1) % 7
            return psb[i]
        i = ps_i["all"]
        ps_i["all"] = (i + 1) % 8
        return psb[i]

    def dma(q, out, in_, stream, **kw):
        eng = {"sp": "sp", "pool": "pool", "act": "act"}[q]
        return P.add(eng, lambda e, o=out, i=in_: e.dma_start(out=o, in_=i), reads=[in_], writes=[out], dma=stream, **kw)

    def act(out, in_, func, bias=0.0, scale=1.0, accum=None, reads=(), big=None):
        r = [in_] + [a for a in (bias, scale) if not isinstance(a, float)] + list(reads)
        w = [out] + ([accum] if accum is not None else [])
        if big is None:
            big = out.free_size() >= 256 and accum is None

        def fn(e):
            kw = {}
            if accum is not None:
                kw["accum_out"] = accum
            return e.activation(out=out, in_=in_, func=func, bias=bias, scale=scale, **kw)
        return P.add("act", fn, reads=r, writes=w, big=big)

    def tt(eng, out, in0, in1, op):
        return P.add(eng, lambda e: e.tensor_tensor(out=out, in0=in0, in1=in1, op=op),
                     reads=[in0, in1], writes=[out], big=out.free_size() >= 256)

    def ts(eng, out, in0, s1, s2, op0, op1=None):
        r = [in0] + [a for a in (s1, s2) if a is not None and not isinstance(a, float)]

        def fn(e):
            if op1 is None:
                return e.tensor_scalar(out=out, in0=in0, scalar1=s1, scalar2=None, op0=op0)
            return e.tensor_scalar(out=out, in0=in0, scalar1=s1, scalar2=s2, op0=op0, op1=op1)
        return P.add(eng, fn, reads=r, writes=[out], big=out.free_size() >= 256)

    def stt(eng, out, in0, s, in1, op0, op1):
        r = [in0, in1] + ([] if isinstance(s, float) else [s])
        return P.add(eng, lambda e: e.scalar_tensor_tensor(out=out, in0=in0, scalar=s, in1=in1, op0=op0, op1=op1),
                     reads=r, writes=[out], big=out.free_size() >= 256)

    def recip(out, in_):
        return P.add("dve", lambda e: e.reciprocal(out=out, in_=in_), reads=[in_], writes=[out], big=out.free_size() >= 256)

    def mm_group(out, pairs):
        r = [a for p in pairs for a in p]

        def fn(e):
            n = len(pairs)
            ins = None
            for i, (l, rr) in enumerate(pairs):
                ins = e.matmul(out, lhsT=l, rhs=rr, start=(i == 0), stop=(i == n - 1))
            return ins
        return P.add("pe", fn, reads=r, writes=[out])

    def transposes(out_ps_bf, in_sb, n, rows=128):
        def fn(e):
            ins = None
            for i in range(n):
                ins = e.transpose(out=out_ps_bf[:, i * 128: i * 128 + rows],
                                  in_=in_sb[0:rows, i * 128:(i + 1) * 128], identity=ident[0:rows, 0:rows])
            return ins
        return P.add("pe", fn, reads=[in_sb[0:rows, :], ident], writes=[out_ps_bf[:, 0:n * 128]])

    def skewed(items, stages, rev=True):
        ns = len(stages)
        for step in range(len(items) + ns - 1):
            for s in (range(ns - 1, -1, -1) if rev else range(ns)):
                i = step - s
                if 0 <= i < len(items):
                    stages[s](items[i])

    epsb = small[:, 190:191]
    epsl = small[:, 191:192]

    def rms_rstd(s, r):
        act(s, s, AF.Sqrt, bias=epsb[0:s.shape[0], :], scale=1.0 / D)
        recip(r, s)

    slot_i = [0]

    def load_chunk(src):
        s = slot_i[0]
        slot_i[0] = (s + 1) % NSLOT
        dma("pool", ring[s], src, "w%d" % s)
        return ring[s]

    dma("sp", cf, cst_f, "const")
    dma("pool", cb, cst_b, "constb")
    dma("sp", gt[0], g4[0], "g0")
    esink = small[:, 0:16]
    act(esink, c_sink, AF.Exp)
    P.add("dve", lambda e: e.memset(epsb, 1e-6), writes=[epsb])
    P.add("dve", lambda e: e.memset(epsl, 1e-5), writes=[epsl])
    P.add("dve", lambda e: e.memset(vA[:, :, :, 64:65], 1.0), writes=[vA[:, :, :, 64:65]])
    TB = o_T
    bvt = f32(TB, 128)
    TB2 = TB + 512
    dma("sp", bvt, bv_bc, "const2")

    sqj = bf(TB2, 1024)
    xh = [bf(TB2 + 2048 + i * 2048, 1024) for i in range(2)]
    ss = small[:, 16:48]

    xs1 = [xs[0], xs[1], gt[1], f32(TB2 + 8192, 1024)]

    def p1_a(b):
        nrows = 128 if b < NB else NMETA
        xt = xs1[b % 4]
        dma("sp", xt[0:nrows, :], xin[b * 128:b * 128 + nrows, :], "xp%d" % (b % 4))
        s = ss[0:nrows, (b % 16) * 2:(b % 16) * 2 + 1]
        r = ss[0:nrows, (b % 16) * 2 + 1:(b % 16) * 2 + 2]
        act(sqj[0:nrows, :], xt[0:nrows, :], AF.Square, accum=s)
        rms_rstd(s, r)

    p1s = {}

    def p1_b(b):
        nrows = 128 if b < NB else NMETA
        xt = xs1[b % 4]
        r = ss[0:nrows, (b % 16) * 2 + 1:(b % 16) * 2 + 2]
        xo = xh[b % 2]
        stt("dve", xo[0:nrows, :], xt[0:nrows, :], r, gt[0][0:nrows, :], ALU.mult, ALU.mult)
        pt = psum()
        ptb = pt[:, :].bitcast(BF16)
        transposes(ptb, xo, 8, rows=nrows)
        p1s[b] = ptb

    def p1_c(b):
        nrows = 128 if b < NB else NMETA
        ptb = p1s.pop(b)
        src = ptb.rearrange("p (c t) -> p c t", c=8)[:, :, 0:nrows]
        act(xT[:, :, b * 128:b * 128 + nrows], src, AF.Copy)

    CH_Q, CH_K, CH_V, CH_A, CH_G, CH_GA, CH_GC = 0, 8, 10, 11, 19, 27, 35
    sg = [f32(TB2 + 16384 + i * 2048, 512) for i in range(2)]

    def proj(wt, t0, t1, pool="all"):
        pt = psum(pool)
        o = pt[:, 0:t1 - t0]
        mm_group(o, [(wt[:, kc, :], xT[:, kc, t0:t1]) for kc in range(8)])
        return o

    assert 3 * NTM * 2 <= 4 * 4096
    tabC = bf(o_X, NTM)
    tabS = bf(o_X + NTM * 2, NTM)
    Pt = bf(o_X + NTM * 4, NTM)
    rtmp = f32(TB2 + 8192, 512)

    def rope_work(X, lo, hi):
        n = hi - lo
        out = []

        def swaps():
            for (a, b_) in ((0, 8), (8, 0), (64, 72), (72, 64)):
                dma("sp", Pt[a:a + 8, lo:hi], X[b_:b_ + 8, 0:n], "rp")
        out.append(swaps)
        for (t0, t1) in tiles(lo, hi):
            def grp(t0=t0, t1=t1):
                tm = rtmp[:, 0:t1 - t0]
                xsl = X[:, t0 - lo:t1 - lo]
                tt("dve", tm, Pt[:, t0:t1], tabS[:, t0:t1], ALU.mult)
                tt("dve", xsl, xsl, tabC[:, t0:t1], ALU.mult)
                tt("dve", xsl, xsl, tm, ALU.add)
            out.append(grp)
        return out

    def k_job(g, wk, t0, t1):
        pk = proj(wk, t0, t1)
        act(kT[:, g, t0:t1], pk, AF.Identity, bias=c_bin[:, CH_K + g:CH_K + g + 1])

    def q_job(c, wq, t0, t1):
        pq = proj(wq, t0, t1)
        act(Bq[:, c, t0 - M0:t1 - M0], pq, AF.Identity, bias=c_bin[:, CH_Q + c:CH_Q + c + 1])

    def v_job(b):
        nrows = 128 if b < NB else NMETA
        pt = psum()
        o = pt[0:nrows, 0:128]
        mm_group(o, [(xT[:, kc, b * 128:b * 128 + nrows], wv[:, kc, :]) for kc in range(8)])
        tt("dve", vA[0:nrows, b, :, 0:64], o.rearrange("p (g e) -> p g e", g=2),
           bvt[0:nrows, :].rearrange("p (g e) -> p g e", g=2), ALU.add)

    NQ0 = 3
    wks = [load_chunk(win_l[CH_K + g]) for g in range(2)]
    wv = load_chunk(win_l[CH_V])
    wq0 = [load_chunk(win_l[CH_Q + c]) for c in range(NQ0)]
    jobs = []
    for g in range(2):
        for (t0, t1) in tiles(0, NTM):
            need = NB if t1 > NT else (t1 - 1) // 128
            jobs.append((need, lambda g=g, t0=t0, t1=t1: k_job(g, wks[g], t0, t1)))
    for c in range(NQ0):
        for (t0, t1) in tiles(M0, NT):
            jobs.append(((t1 - 1) // 128, lambda c=c, t0=t0, t1=t1: q_job(c, wq0[c], t0, t1)))
    jobs.sort(key=lambda j: j[0])
    for step in range(NB + 3):
        if 0 <= step - 2 <= NB:
            p1_c(step - 2)
        if 0 <= step - 1 <= NB:
            p1_b(step - 1)
        if step <= NB:
            p1_a(step)
        done = step - 2
        if 0 <= done <= NB:
            v_job(done)
            while jobs and jobs[0][0] <= done:
                jobs.pop(0)[1]()
    assert not jobs
    dma("pool", tabC, ropeC, "rtc")
    dma("pool", tabS, ropeS, "rts")
    P.add("dve", lambda e: e.memset(Pt, 0.0), writes=[Pt])
    for c in range(NQ0, 8):
        wq = load_chunk(win_l[CH_Q + c])
        for (t0, t1) in tiles(M0, NT):
            q_job(c, wq, t0, t1)
    rope_list = []
    for (X, lo, hi) in [(kT[:, g, 0:NTM], 0, NTM) for g in range(2)] + [(Bq[:, c, :], M0, NT) for c in range(8)]:
        rope_list += rope_work(X, lo, hi)
    rope_rate = len(rope_list) / float(8 * len(tiles(M0, NT)) - 2)
    rope_credit = [0.0]

    for c in range(8):
        wa = load_chunk(win_l[CH_A + c])
        wg = load_chunk(win_l[CH_G + c])
        for i, (t0, t1) in enumerate(tiles(M0, NT)):
            pa = proj(wa, t0, t1)
            pg = proj(wg, t0, t1)
            s_ = sg[i % 2][:, 0:t1 - t0]
            act(s_, pg, AF.Sigmoid, bias=c_bin[:, CH_G + c:CH_G + c + 1])
            stt("dve", Cg[:, c, t0 - M0:t1 - M0], pa, c_bin[:, CH_A + c:CH_A + c + 1], s_, ALU.add, ALU.mult)
            rope_credit[0] += rope_rate
            while rope_credit[0] >= 1.0 - 1e-9 and rope_list:
                rope_list.pop(0)()
                rope_credit[0] -= 1.0
        tt("dve", Cg[:, c, 0:128], Cg[:, c, 0:128], padmask, ALU.mult)
    while rope_list:
        rope_list.pop(0)()

    dgbs = [bf(o_X + i * 8192, 31 * 128).rearrange("p (k m) -> p k m", k=31) for i in range(2)]
    P.add("dve", lambda e: e.memset(Dc[:, :, 0:32], 0.0), writes=[Dc[:, :, 0:32]])
    CT = tiles(M0 + 32, NT)
    NDT = 8
    cacc = f32(TB2 + 16384, 512)

    def conv_tile(c, ti):
        dgb = dgbs[c % 2]
        if ti == 0:
            idb = bass.AP(tensor=ident.tensor, offset=ident.offset, ap=[list(ident.ap[0]), [0, 31], [1, 128]])
            wsl = c_cw[:, c, :]
            wb = bass.AP(tensor=wsl.tensor, offset=wsl.offset, ap=[list(wsl.ap[0]), [1, 31], [0, 128]])
            P.add("dve", lambda e: e.tensor_tensor(out=dgb, in0=idb, in1=wb, op=ALU.mult),
                  reads=[ident, wsl], writes=[dgb], big=True)
        t0, t1 = CT[ti]
        n = t1 - t0
        pc = psb[6][:, 0:n]

        def src(k):
            return Cg[:, c, t0 - M0 - 30 + k:t1 - M0 - 30 + k]
        aD = cacc[:, 0:n]
        ts("dve", aD, src(0), c_cw[:, c, 0:1], None, ALU.mult)
        for k in range(1, NDT):
            stt("dve", aD, src(k), c_cw[:, c, k:k + 1], aD, ALU.mult, ALU.add)
        mm_group(pc, [(dgb[:, k, :], src(k)) for k in range(NDT, 31)])
        stt("dve", Dc[:, c, t0 - M0:t1 - M0], pc, c_cb[:, c:c + 1], aD, ALU.add, ALU.add)

    conv_items = [(c, ti) for c in range(8) for ti in range(len(CT))]

    PTW = 12
    pts = [bf(TB2 + i * 1024, 512) for i in range(PTW)]
    osbs = [bf(TB2 + 12288 + i * 2048, 1024) for i in range(2)]
    dens = [small[:, 48 + 16 * i:64 + 16 * i] for i in range(2)]
    rdens = [small[:, 80 + 16 * i:96 + 16 * i] for i in range(2)]
    pti = [0]
    HB = [(0, 7), (7, 14), (14, 16)]

    def o_region(b, h):
        for bi, (h0, h1) in enumerate(HB):
            if h0 <= h < h1:
                return psb[bi][:, (h - h0) * 65:(h - h0) * 65 + 65]

    att_jobs = {}

    def att_a(u):
        b, g = u
        q0 = b * 128 - M0
        jobs = {h: [] for h in range(8 * g, 8 * g + 8)}
        for kc in range(3):
            if kc == 0:
                nk, ktok, vblk = NMETA, NT, NB
                msk = m1_meta if b == 1 else None
            elif kc == 1:
                nk, ktok, vblk = 128, (b - 1) * 128, b - 1
                msk = m1_prev if b == 1 else (m2_prev if b == 2 else m_prev)
            else:
                nk, ktok, vblk = 128, b * 128, b
                msk = m1_same if b == 1 else m_same
            for par in range(2):
                p0 = par * 64
                pt = psum("att")
                o = pt[0:nk, :]
                rhs = Bq[p0:p0 + 64, 4 * g:4 * g + 4, q0:q0 + 128]
                lhsT = kT[p0:p0 + 64, g, ktok:ktok + nk]
                P.add("pe", lambda e, o=o, l=lhsT, r=rhs: e.matmul(o, lhsT=l, rhs=r, start=True, stop=True),
                      reads=[lhsT, rhs], writes=[o])
                pT = pts[pti[0] % PTW]
                pti[0] += 1
                act(pT[0:nk, :], o, AF.Exp, scale=HD ** -0.5)
                if msk is not None:
                    pv = pT[0:nk, :].rearrange("p (j q) -> p j q", j=4)
                    mk = msk[0:nk, :]
                    mb = bass.AP(tensor=mk.tensor, offset=mk.offset, ap=[list(mk.ap[0]), [0, 4], [1, 128]])
                    P.add("dve", lambda e, pv=pv, mb=mb: e.tensor_tensor(out=pv, in0=pv, in1=mb, op=ALU.mult),
                          reads=[pv, mk], writes=[pv], big=True)
                for j in range(4):
                    h = 8 * g + 2 * j + par
                    jobs[h].append((pT[0:nk, j * 128:(j + 1) * 128], vA[0:nk, vblk, g, :]))
        att_jobs[u] = jobs

    def att_b(u):
        b, g = u
        jobs = att_jobs.pop(u)
        for h in range(8 * g, 8 * g + 8):
            mm_group(o_region(b, h), jobs[h])

    def att_f(u):
        b, g = u
        if g != 1:
            return
        q0 = b * 128 - M0
        den, rden, osb = dens[b % 2], rdens[b % 2], osbs[b % 2]
        for bi, (h0, h1) in enumerate(HB):
            nh = h1 - h0
            ov = psb[bi][:, 0:nh * 65].rearrange("p (h e) -> p h e", h=nh)
            tt("dve", den[:, h0:h1], ov[:, :, 64], esink[:, h0:h1], ALU.add)
        recip(rden, den)
        for bi, (h0, h1) in enumerate(HB):
            nh = h1 - h0
            ov = psb[bi][:, 0:nh * 65].rearrange("p (h e) -> p h e", h=nh)
            rd = rden[:, h0:h1]
            rb = bass.AP(tensor=rd.tensor, offset=rd.offset, ap=[list(rd.ap[0]), [1, nh], [0, 64]])
            ob = osb[:, h0 * 64:h1 * 64].rearrange("p (h e) -> p h e", h=nh)
            P.add("dve", lambda e, ob=ob, ov=ov, rb=rb: e.tensor_tensor(out=ob, in0=ov[:, :, 0:64], in1=rb, op=ALU.mult),
                  reads=[ov, rd], writes=[ob], big=True)
        ptb = psb[7].bitcast(BF16)
        transposes(ptb, osb, 8)
        act(Bq[:, :, q0:q0 + 128], ptb.rearrange("p (c t) -> p c t", c=8), AF.Copy)

    LW = 256
    sq = [bf(o_X + i * 512, LW) for i in range(2)]
    mean = f32(o_X + 1024, LW)
    rstd = f32(o_X + 2048, LW)
    msq = f32(o_X + 3072, LW)
    tln = [f32(o_X + 4096 + i * 1024, LW) for i in range(2)]

    def ln_item(t0, t1):
        n = t1 - t0
        p1 = psum("ln")[:, 0:n]
        mm_group(p1, [(ones, Dc[:, c, t0 - M0:t1 - M0]) for c in range(8)])
        p2 = psum("ln")[:, 0:n]
        for c in range(8):
            tt("dve", sq[c % 2][:, 0:n], Dc[:, c, t0 - M0:t1 - M0], Dc[:, c, t0 - M0:t1 - M0], ALU.mult)
            P.add("pe", lambda e, p2=p2, c=c, s_=sq[c % 2][:, 0:n]: e.matmul(p2, lhsT=ones, rhs=s_, start=(c == 0), stop=(c == 7)),
                  reads=[ones, sq[c % 2][:, 0:n]] + ([p2] if c else []), writes=[p2])
        ts("dve", mean[:, 0:n], p1, 1.0 / D, None, ALU.mult)
        tt("dve", msq[:, 0:n], mean[:, 0:n], mean[:, 0:n], ALU.mult)
        stt("dve", rstd[:, 0:n], p2, 1.0 / D, msq[:, 0:n], ALU.mult, ALU.subtract)
        act(rstd[:, 0:n], rstd[:, 0:n], AF.Sqrt, bias=epsl, big=True)
        recip(rstd[:, 0:n], rstd[:, 0:n])
        for c in range(8):
            t_ = tln[c % 2][:, 0:n]
            dsl = Dc[:, c, t0 - M0:t1 - M0]
            tt("dve", t_, dsl, mean[:, 0:n], ALU.subtract)
            tt("dve", dsl, t_, rstd[:, 0:n], ALU.mult)

    def silu_cols(t0, t1):
        for c in range(8):
            dsl = Dc[:, c, t0 - M0:t1 - M0]
            act(dsl, dsl, AF.Silu, bias=c_lb[:, c:c + 1], scale=c_lg[:, c:c + 1], big=True)

    ln_items = tiles(M0, NT, LW)
    ln_done = []
    units = [(b, g) for b in range(1, NB) for g in range(2)]
    rate = len(conv_items) / (0.62 * len(units))
    credit = 0.0
    for k in range(len(units) + 1):
        if k < len(units):
            att_a(units[k])
        if k >= 1:
            u = units[k - 1]
            att_b(u)
            credit += rate
            while credit >= 1.0 - 1e-9 and conv_items:
                conv_tile(*conv_items.pop(0))
                credit -= 1.0
            att_f(u)
            if not conv_items and ln_items:
                lt = ln_items.pop(0)
                ln_item(*lt)
                ln_done.append(lt)
    while conv_items:
        conv_tile(*conv_items.pop(0))
    if ln_done:
        for c in range(8):
            for (t0, t1) in tiles(ln_done[0][0], ln_done[-1][1]):
                dsl = Dc[:, c, t0 - M0:t1 - M0]
                act(dsl, dsl, AF.Silu, bias=c_lb[:, c:c + 1], scale=c_lg[:, c:c + 1])

    sga = [f32(TB2 + i * 2048, 512) for i in range(2)]
    sgc = [f32(TB2 + 4096 + i * 2048, 512) for i in range(2)]
    t1b = [f32(TB2 + 8192 + i * 2048, 512) for i in range(2)]
    t2b = [f32(TB2 + 12288 + i * 2048, 512) for i in range(2)]
    for c in range(8):
        w_ap = load_chunk(wap_l[c])
        w_cp = load_chunk(wcp_l[c])
        w_ga = load_chunk(win_l[CH_GA + c])
        w_gc = load_chunk(win_l[CH_GC + c])
        for i, (t0, t1) in enumerate(tiles(M0, NT)):
            n = t1 - t0
            a0, a1 = t0 - M0, t1 - M0
            if c == 0:
                while ln_items and ln_items[0][0] < t1:
                    lt = ln_items.pop(0)
                    ln_item(*lt)
                    silu_cols(*lt)
            pga = proj(w_ga, t0, t1)
            pgc = proj(w_gc, t0, t1)
            pat = psum()[:, 0:n]
            mm_group(pat, [(w_ap[:, kc, :], Bq[:, kc, a0:a1]) for kc in range(8)])
            pcv = psum()[:, 0:n]
            mm_group(pcv, [(w_cp[:, kc, :], Dc[:, kc, a0:a1]) for kc in range(8)])
            s1 = sga[i % 2][:, 0:n]
            s2 = sgc[i % 2][:, 0:n]
            act(s1, pga, AF.Sigmoid, bias=c_bin[:, CH_GA + c:CH_GA + c + 1])
            act(s2, pgc, AF.Sigmoid, bias=c_bin[:, CH_GC + c:CH_GC + c + 1])
            u1 = t1b[i % 2][:, 0:n]
            u2 = t2b[i % 2][:, 0:n]
            tt("dve", u1, pat, s1, ALU.mult)
            stt("dve", u2, pcv, c_bcp[:, c:c + 1], s2, ALU.add, ALU.mult)
            tt("dve", Cg[:, c, a0:a1], u1, u2, ALU.add)

    if SMALL:
        o_WD = o_E
        o_WO = o_E + NF * D * 2
        o_H = o_WO + 8 * D * 2
    else:
        o_WD = o_W - NF * D * 2
        o_WO = o_WD - 8 * D * 2
        o_H = o_C
        assert o_WO >= o_B
    wo = bf(o_WO, 8 * D).rearrange("p (k n) -> p k n", k=8)
    dma("pool", wo, wout_l, "wo")
    dma("sp", gt[0], g4[1], "g0")
    dma("sp", gt[1], g4[2], "g1")
    wd = bf(o_WD, NF * D).rearrange("p (f n) -> p f n", f=NF)
    yt = [f32(TB2 + i * 4096, 1024) for i in range(2)]
    sqj7 = bf(TB2 + 8192, 1024)
    xh7 = [bf(TB2 + 10240 + i * 2048, 1024) for i in range(2)]
    ss7 = small[:, 112:144]
    p7 = {}

    p7pair = [0]
    p7tb = [0]

    def p7_a(b):
        a0 = b * 128 - M0
        i = p7pair[0]
        p7pair[0] = (i + 1) % 3
        pAB = pst[:, i * 1024:(i + 1) * 1024]
        p7[b] = pAB
        for half in range(2):
            mm_group(pAB[:, half * 512:(half + 1) * 512],
                     [(Cg[:, kc, a0:a0 + 128], wo[:, kc, half * 512:(half + 1) * 512]) for kc in range(8)])
        xt = xs[b % 2]
        dma("sp", xt, xin[b * 128:(b + 1) * 128, :], "x%d" % (b % 2))

    def p7_b(b):
        j = b % 8
        s = ss7[:, 4 * j:4 * j + 1]
        act(sqj7, p7[b], AF.Square, accum=s)
        act(s, s, AF.Sqrt, bias=epsb, scale=1.0 / D)

    def p7_c(b):
        pAB = p7.pop(b)
        xt = xs[b % 2]
        j = b % 8
        s, r = ss7[:, 4 * j:4 * j + 1], ss7[:, 4 * j + 1:4 * j + 2]
        h1 = yt[b % 2]
        recip(r, s)
        stt("dve", h1, pAB, r, gt[0], ALU.mult, ALU.mult)
        tt("dve", h1, h1, xt, ALU.add)
        if b >= 2:
            dma("pool", h1s[(b - 2) * 128:(b - 1) * 128, :], h1, "h%d" % (b % 2), extra_w=[("h1s", b)])

    def p7_d(b):
        j = b % 8
        s2 = ss7[:, 4 * j + 2:4 * j + 3]
        act(sqj7, yt[b % 2], AF.Square, accum=s2)
        act(s2, s2, AF.Sqrt, bias=epsb, scale=1.0 / D)

    def p7_e(b):
        j = b % 8
        s2, r2 = ss7[:, 4 * j + 2:4 * j + 3], ss7[:, 4 * j + 3:4 * j + 4]
        recip(r2, s2)
        stt("dve", xh7[b % 2], yt[b % 2], r2, gt[1], ALU.mult, ALU.mult)

    def p7_f(b):
        i = p7tb[0]
        p7tb[0] = (i + 1) % 2
        ptb = psb[6 + i].bitcast(BF16)
        transposes(ptb, xh7[b % 2], 8)
        p7[("t", b)] = ptb

    def p7_g(b):
        ptb = p7.pop(("t", b))
        act(xT[:, :, b * 128:(b + 1) * 128], ptb.rearrange("p (c t) -> p c t", c=8), AF.Copy)

    skewed(list(range(1, NB)), [p7_a, p7_b, p7_c, p7_d, p7_e, p7_f, p7_g])

    SZ_H = NF * HTOK * 2
    hT = bf(o_H, NF * HTOK).rearrange("p (f t) -> p f t", f=NF)
    assert SMALL or o_C + SZ_H <= o_WD, (SZ_H, o_WD)
    dma("sp", gt[0], g4[3], "g0")
    NAC = 3
    ac = [[f32(TB2 + (2 * s_ + k) * 2048, 512) for k in range(2)] for s_ in range(NAC)]
    cys = [[small[:, 164 + 4 * s_ + 2 * k:166 + 4 * s_ + 2 * k] for k in range(2)] for s_ in range(NAC)]
    cy0 = [small[:, 176 + 2 * i:178 + 2 * i] for i in range(4)]
    ot = [f32(TB2 + 12288 + i * 4096, 1024) for i in range(2)]
    sqj9 = bf(o_X + 12288, 1024)
    ss9 = small[:, 148:164]
    minib = psb[7]
    mini_i = [0]
    ffn_state = {}

    for hf in range(NHALF):
        T0 = OWN0 + hf * HTOK
        TL = tiles(T0, T0 + HTOK)
        units = [(f, ti) for f in range(NF) for ti in range(len(TL))]
        wts = {}

        def f_a(u, T0=T0, TL=TL, hf=hf):
            f, ti = u
            t0, t1 = TL[ti]
            n = t1 - t0
            if ti == 0:
                wts[f] = (load_chunk(wup_l[f]), load_chunk(wup_l[NF + f]))
                if hf == 0:
                    dma("pool", wd[:, f, :], wdn_l[:, f, :], "wd")
            ui = f * len(TL) + ti
            sl = ui % NAC
            st = {}
            for k, ch in enumerate((f, NF + f)):
                wt = wts[f][k]
                if ti == 0:
                    mi = mini_i[0]
                    mini_i[0] += 1
                    cyp = minib[:, 2 * (mi % 256):2 * (mi % 256) + 2]
                    mm_group(cyp, [(wt[:, kc, :], xT[:, kc, t0 - 2:t0]) for kc in range(8)])
                    cy = cy0[mi % 4]
                    act(cy, cyp, AF.Copy, big=False)
                else:
                    cy = ffn_state[(f, ti - 1)]["cy"][k]
                pu = proj(wt, t0, t1, "ffn")
                a_ = ac[sl][k][:, 0:n]
                act(a_, pu, AF.Identity, bias=c_fb[:, ch:ch + 1], scale=c_fw[:, ch, 2:3])
                if ti + 1 < len(TL):
                    cn = cys[sl][k]
                    act(cn, pu[:, n - 2:n], AF.Copy, big=False)
                    st.setdefault("cy", {})[k] = cn
                w1, w0 = c_fw[:, ch, 1:2], c_fw[:, ch, 0:1]
                stt("dve", a_[:, 1:n], pu[:, 0:n - 1], w1, a_[:, 1:n], ALU.mult, ALU.add)
                stt("dve", a_[:, 2:n], pu[:, 0:n - 2], w0, a_[:, 2:n], ALU.mult, ALU.add)
                stt("dve", a_[:, 0:1], cy[:, 1:2], w1, a_[:, 0:1], ALU.mult, ALU.add)
                stt("dve", a_[:, 0:2], cy[:, 0:2], w0, a_[:, 0:2], ALU.mult, ALU.add)
            ffn_state[(f, ti)] = st

        def f_b(u, T0=T0, TL=TL):
            f, ti = u
            t0, t1 = TL[ti]
            n = t1 - t0
            ui = f * len(TL) + ti
            sl = ui % NAC
            ag = ac[sl][0][:, 0:n]
            av = ac[sl][1][:, 0:n]
            act(ag, ag, AF.Silu)
            tt("dve", hT[:, f, t0 - T0:t1 - T0], ag, av, ALU.mult)

        skewed(units, [f_a, f_b], rev=False)

        p9 = {}

        def p9_a(bb, hf=hf):
            blk = hf * (HTOK // 128) + bb
            pAB = psum2()
            p9[bb] = pAB
            for half in range(2):
                mm_group(pAB[:, half * 512:(half + 1) * 512],
                         [(hT[:, f, bb * 128:(bb + 1) * 128], wd[:, f, half * 512:(half + 1) * 512]) for f in range(NF)])
            ht = xs[blk % 2]
            dma("sp", ht, h1s[blk * 128:(blk + 1) * 128, :], "x%d" % (blk % 2), extra_r=[("h1s", blk + 2)])
            s = ss9[:, 2 * (blk % 8):2 * (blk % 8) + 1]
            r = ss9[:, 2 * (blk % 8) + 1:2 * (blk % 8) + 2]
            act(sqj9, pAB, AF.Square, accum=s)
            rms_rstd(s, r)

        def p9_b(bb, hf=hf):
            blk = hf * (HTOK // 128) + bb
            pAB = p9.pop(bb)
            ht = xs[blk % 2]
            r = ss9[:, 2 * (blk % 8) + 1:2 * (blk % 8) + 2]
            o_ = ot[blk % 2]
            stt("dve", o_, pAB, r, gt[0], ALU.mult, ALU.mult)
            tt("dve", o_, o_, ht, ALU.add)
            dma("pool", y[blk * 128:(blk + 1) * 128, :], o_, "y%d" % (blk % 2))

        skewed(list(range(HTOK // 128)), [p9_a, p9_b])

    P.finalize_and_emit(final_streams=["y0", "y1"])
    return nc, P


def _chunk(w, cols):
    sub = w[:, cols]
    return sub.reshape(8, 128, 128).transpose(1, 0, 2)


def host_prep(inp, NOWN=16, n_cores=8):
    f = np.float32
    x = np.asarray(inp["x"], f)
    B, S, _ = x.shape
    assert S == 2 * NOWN * 128 and B * 2 == n_cores
    NB = NOWN + 2
    NT = NB * 128
    NTM = NT + NMETA
    meta = np.asarray(inp["meta_tokens"], f)
    w_in = np.asarray(inp["w_in"], f)[0]
    b_in = np.asarray(inp["b_in"], f)[0]
    ar = np.arange(128)
    cols = [c * 128 + ar for c in range(8)]
    cols += [1024 + 64 * g + (ar % 64) for g in range(2)]
    cols += [1152 + ar]
    cols += [1280 + c * 128 + ar for c in range(8)]
    cols += [2304 + c * 128 + ar for c in range(8)]
    cols += [3328 + c * 128 + ar for c in range(8)]
    cols += [4352 + c * 128 + ar for c in range(8)]
    win_l = np.ascontiguousarray(np.stack([_chunk(w_in, c) for c in cols]))
    bin_l = np.stack([b_in[c] for c in cols], axis=1)
    wap = np.asarray(inp["w_attn_proj"], f)[0]
    wcp = np.asarray(inp["w_conv_proj"], f)[0]
    wap_l = np.ascontiguousarray(np.stack([_chunk(wap, c * 128 + ar) for c in range(8)]))
    wcp_l = np.ascontiguousarray(np.stack([_chunk(wcp, c * 128 + ar) for c in range(8)]))
    wout_l = np.ascontiguousarray(np.asarray(inp["w_out"], f)[0].reshape(8, 128, D).transpose(1, 0, 2))
    w_up = np.asarray(inp["w_up"], f)[0]
    wup_l = np.ascontiguousarray(np.stack([_chunk(w_up, c * 128 + ar) for c in range(44)]))
    wdn_l = np.ascontiguousarray(np.asarray(inp["w_down"], f)[0].reshape(NF, 128, D).transpose(1, 0, 2))

    def fm(v):
        return np.asarray(v, f).reshape(-1, 128).T

    cst_f = np.zeros((128, 576), f)
    cst_f[:, 0:43] = bin_l
    cw = np.asarray(inp["conv_dw_w"], f)[0]
    cst_f[:, 64:64 + 248] = cw.reshape(31, 8, 128).transpose(2, 1, 0).reshape(128, 248)
    cst_f[:, 320:328] = fm(inp["conv_dw_b"][0])
    cst_f[:, 328:336] = fm(inp["conv_ln_g"][0])
    cst_f[:, 336:344] = fm(inp["conv_ln_b"][0])
    cst_f[:, 344:352] = fm(inp["b_conv_proj"][0])
    fw = np.asarray(inp["ffn_dw_w"], f)[0]
    cst_f[:, 352:352 + 132] = fw.reshape(3, 44, 128).transpose(2, 1, 0).reshape(128, 132)
    cst_f[:, 484:528] = fm(inp["ffn_dw_b"][0])
    cst_f[:, 528:544] = np.broadcast_to(np.asarray(inp["attn_sinks"], f)[0][None, :], (128, 16))
    g4 = np.stack([np.broadcast_to(np.asarray(inp[k], f)[0][None, :], (128, D))
                   for k in ("norm_pre_mix", "norm_post_mix", "norm_pre_ffn", "norm_post_ffn")])
    g4 = np.ascontiguousarray(g4)
    bv_bc = np.ascontiguousarray(np.broadcast_to(b_in[1152:1280][None, :], (128, 128)))

    kk = np.arange(128)[:, None]
    qq = np.arange(128)[None, :]
    same = (kk <= qq).astype(f)
    prev = (kk > qq).astype(f)
    inv_freq = (500000.0 ** (-np.arange(8, dtype=f) * f(2.0) / f(16))).astype(f)
    maps = []
    for c in range(n_cores):
        b, half = c // 2, c % 2
        xin = np.zeros((NTM, D), f)
        if half == 0:
            xin[240:256] = meta
            xin[256:NT] = x[b, 0:NOWN * 128]
        else:
            xin[0:NT] = x[b, NOWN * 128 - 256:2 * NOWN * 128]
        xin[NT:NTM] = meta
        pos = half * NOWN * 128 + np.arange(NT) - 240
        pos = np.maximum(pos, 0)
        pos = np.concatenate([pos, np.arange(NMETA)]).astype(f)
        ang = pos[None, :] * inv_freq[:, None]
        cosv, sinv = np.cos(ang).astype(f), np.sin(ang).astype(f)
        C = np.ones((128, NTM), f)
        S_ = np.zeros((128, NTM), f)
        for base in (0, 64):
            C[base:base + 8] = cosv
            C[base + 8:base + 16] = cosv
            S_[base:base + 8] = -sinv
            S_[base + 8:base + 16] = sinv
        cb = np.zeros((128, 1152), f)
        cb[:, 0:128] = np.eye(128, dtype=f)
        cb[:, 128:256] = 1.0
        cb[:, 256:384] = same
        cb[:, 384:512] = prev
        if half == 0:
            mm = np.ones((128, 128), f)
            m_idx = np.arange(128)[:, None]
            mq = (np.arange(128)[None, :] - 112)
            mm = np.where(mq >= 0, (m_idx <= mq), True).astype(f)
            cb[:, 896:1024] = mm
            pm = np.zeros((128, 128), f)
            pm[:, 112:128] = 1.0
            cb[:, 1024:1152] = pm
        else:
            cb[:, 512:640] = same
            cb[:, 640:768] = prev
            cb[:, 768:896] = prev
            cb[:, 896:1024] = 1.0
            cb[:, 1024:1152] = 1.0
        maps.append({
            "xin": xin, "win_l": win_l, "wap_l": wap_l, "wcp_l": wcp_l, "wout_l": wout_l,
            "wup_l": wup_l, "wdn_l": wdn_l, "ropeC": C, "ropeS": S_, "cst_f": cst_f,
            "cst_b": cb, "g4": g4, "bv_bc": bv_bc,
        })
    return maps


_CACHE = {}


def kernel(**inputs):
    NOWN = 16
    maps = host_prep(inputs, NOWN, 8)
    if "nc" not in _CACHE:
        _CACHE["nc"] = build(NOWN)[0]
    nc = _CACHE["nc"]
    res = run_bass_kernel_spmd(nc, maps, core_ids=list(range(8)))
    B = 4
    out = np.zeros((B, 2 * NOWN * 128, D), np.float32)
    for c in range(8):
        out[c // 2, (c % 2) * NOWN * 128:(c % 2 + 1) * NOWN * 128] = res.results[c]["y"]
    return out
```

```python
import numpy as np
import concourse.bass as bass
import concourse.mybir as mybir
from concourse.bass_utils import run_bass_kernel_spmd

dt = mybir.dt
F32 = dt.float32
BF16 = dt.bfloat16
AF = mybir.ActivationFunctionType
ALU = mybir.AluOpType
AX = mybir.AxisListType

D = 1024
NMETA = 16
HD = 64
FFN = 2816
NF = FFN // 128
CONVK = 31
IN_W = 5376
CELL = 256
ESZ = {F32: 4, BF16: 2}


def _esz(d):
    if d == F32:
        return 4
    if d == BF16:
        return 2
    raise ValueError(d)


class Op:
    __slots__ = ("eng", "fn", "idx", "deps", "dma", "val", "inc", "cnt", "big", "waits")


class Prog:
    ENG = ("pe", "act", "dve", "pool", "sp")

    def __init__(self, nc):
        self.nc = nc
        self.ops = []
        self.lastw = {}
        self.readers = {}
        self.streams = {}
        self.always_sync_same = False

    def cells(self, ap):
        t = ap.tensor
        cls = type(t).__name__
        if "DRam" in cls or "DRAM" in cls:
            return ()
        name = t.name
        is_psum = "PSum" in cls or "Psum" in cls or "PSUM" in cls
        if is_psum:
            self._psum = True
        esz = _esz(ap.dtype)
        a = [list(x) for x in ap.ap]
        ps = a[0][0]
        off = ap.offset % ps if ps else ap.offset
        dims = a[1:]
        if not dims:
            dims = [[1, 1]]
        ls, ln = dims[-1]
        span = (ln - 1) * abs(ls) + 1
        starts = [off]
        for s, n in dims[:-1]:
            starts = [b + i * s for b in starts for i in range(n)]
        out = set()
        CELL = 8 if name == "small" else (2048 if is_psum else 256)
        for b in starts:
            lo = (b * esz) // CELL
            hi = ((b + span) * esz - 1) // CELL
            for c in range(lo, hi + 1):
                out.add((name, c))
        return out

    def add(self, eng, fn, reads=(), writes=(), big=False, dma=None, extra_r=(), extra_w=()):
        op = Op()
        op.eng, op.fn, op.idx, op.big, op.dma = eng, fn, len(self.ops), big, dma
        op.inc = False
        op.cnt = 0
        deps = set()
        rc = set(extra_r)
        wc = set(extra_w)
        for ap in reads:
            self._psum = False
            cs = set(self.cells(ap))
            if self._psum:
                wc |= cs
            else:
                rc |= cs
        for ap in writes:
            wc |= set(self.cells(ap))
        for c in rc:
            w = self.lastw.get(c)
            if w is not None:
                deps.add(w)
        for c in wc:
            w = self.lastw.get(c)
            if w is not None:
                deps.add(w)
            for r in self.readers.get(c, ()):
                deps.add(r)
        deps.discard(op.idx)
        op.deps = deps
        for c in wc:
            self.lastw[c] = op.idx
            self.readers[c] = []
        for c in rc:
            if c not in wc:
                self.readers.setdefault(c, []).append(op.idx)
        if dma is not None:
            st = self.streams.setdefault(dma, [None, 0])
            st[1] += 16
            op.val = st[1]
        self.ops.append(op)
        return op

    def finalize_and_emit(self, final_streams=()):
        nc = self.nc
        ops = self.ops
        for op in ops:
            for d in op.deps:
                dop = ops[d]
                if dop.dma is None:
                    if dop.eng == op.eng and op.dma is None:
                        if dop.eng == "pe":
                            continue
                        if dop.big and not self.always_sync_same:
                            continue
                    dop.inc = True
        cnt = {e: 0 for e in self.ENG}
        for op in ops:
            if op.dma is None and op.inc:
                cnt[op.eng] += 1
                op.cnt = cnt[op.eng]
        self.sems = {e: nc.alloc_semaphore("s_" + e) for e in self.ENG}
        for name, st in self.streams.items():
            st[0] = nc.alloc_semaphore("d_" + name)
        known = {e: {} for e in self.ENG}
        for op in ops:
            need = {}
            for d in op.deps:
                dop = ops[d]
                if dop.dma is not None:
                    key = ("d", dop.dma)
                    v = dop.val
                else:
                    if dop.eng == op.eng and op.dma is None:
                        if dop.eng == "pe":
                            continue
                        if dop.big and not self.always_sync_same:
                            continue
                    key = ("e", dop.eng)
                    v = dop.cnt
                if need.get(key, 0) < v:
                    need[key] = v
            w = []
            kn = known[op.eng]
            for key, v in need.items():
                if kn.get(key, 0) >= v:
                    continue
                kn[key] = v
                sem = self.streams[key[1]][0] if key[0] == "d" else self.sems[key[1]]
                w.append((sem, v))
            op.waits = w
        by = {e: [o for o in ops if o.eng == e] for e in self.ENG}
        handles = {"pe": "tensor", "act": "scalar", "dve": "vector", "pool": "gpsimd", "sp": "sync"}
        stats = {e: len(by[e]) for e in self.ENG}
        self.stats = stats

        def emit(ename, e):
            for op in by[ename]:
                for sem, v in op.waits:
                    e.wait_ge(sem, v)
                ins = op.fn(e)
                if op.dma is not None:
                    ins.then_inc(self.streams[op.dma][0], 16)
                elif op.inc:
                    ins.then_inc(self.sems[ename], 1)
            if ename == "sp":
                for s in final_streams:
                    st = self.streams[s]
                    e.wait_ge(st[0], st[1])

        with nc.Block() as block:
            @block.tensor
            def _(e):
                emit("pe", e)

            @block.scalar
            def _(e):
                emit("act", e)

            @block.vector
            def _(e):
                emit("dve", e)

            @block.gpsimd
            def _(e):
                emit("pool", e)

            @block.sync
            def _(e):
                emit("sp", e)


def tiles(lo, hi, w=512):
    out = []
    t = lo
    while t < hi:
        out.append((t, min(t + w, hi)))
        t += w
    return out


def build(NOWN=16, dbg=False):
    NB = NOWN + 2
    NT = NB * 128
    NTM = NT + NMETA
    M0 = 128
    MW = NT - M0
    OWN0 = 256
    NHALF = 2 if NOWN >= 8 else 1
    HTOK = NOWN * 128 // NHALF

    nc = bass.Bass("TRN2", target_bir_lowering=False)
    P = Prog(nc)

    def din(name, shape, d=F32):
        return nc.dram_tensor(name, list(shape), d, kind="ExternalInput").ap()

    xin = din("xin", [NTM, D])
    win_l = din("win_l", [43, 128, 8, 128])
    wap_l = din("wap_l", [8, 128, 8, 128])
    wcp_l = din("wcp_l", [8, 128, 8, 128])
    wout_l = din("wout_l", [128, 8, D])
    wup_l = din("wup_l", [44, 128, 8, 128])
    wdn_l = din("wdn_l", [128, NF, D])
    ropeC = din("ropeC", [128, NTM])
    ropeS = din("ropeS", [128, NTM])
    cst_f = din("cst_f", [128, 576])
    cst_b = din("cst_b", [128, 1152])
    g4 = din("g4", [4, 128, D])
    bv_bc = din("bv_bc", [128, 128])
    y = nc.dram_tensor("y", [NOWN * 128, D], F32, kind="ExternalOutput").ap()
    h1s = nc.dram_tensor("h1s", [NOWN * 128, D], F32).ap()

    XW = ((NTM + 127) // 128) * 128
    SZ_A = 8 * XW * 2
    SZ_B = 8 * MW * 2
    o_A = 0
    o_C = o_A + SZ_A
    o_B = o_C + SZ_B
    o_D = o_B + SZ_B
    o_W = o_D + SZ_B
    NSLOT = 6
    o_KV = o_W + NSLOT * 2048
    SZ_K = 2 * NTM * 2
    SZ_V = (NB + 1) * 130 * 2
    o_X = ((o_KV + SZ_K + SZ_V + 255) // 256) * 256
    o_T = o_X + 4 * 4096
    SZ_T = 20992
    o_S = o_T + SZ_T
    SZ_S = 576 * 4 + 1152 * 2
    TOTAL = o_S + SZ_S
    SMALL = NOWN < 16
    if SMALL:
        o_E = TOTAL
        TOTAL += NF * D * 2 + 8 * D * 2 + NF * HTOK * 2
    assert TOTAL <= 212040, TOTAL
    arena = nc.alloc_sbuf_tensor("arena", [128, TOTAL // 2], BF16)

    def bf(off, n):
        assert off % 2 == 0
        return arena[:, off // 2: off // 2 + n]

    def f32(off, n):
        assert off % 4 == 0
        return arena[:, off // 2: off // 2 + 2 * n].bitcast(F32)

    def v3(ap, a):
        return ap.rearrange("p (a b) -> p a b", a=a)

    xT = v3(bf(o_A, 8 * XW), 8)
    Bq = v3(bf(o_B, 8 * MW), 8)
    Cg = v3(bf(o_C, 8 * MW), 8)
    Dc = v3(bf(o_D, 8 * MW), 8)
    kT = v3(bf(o_KV, 2 * NTM), 2)
    vA = bf(o_KV + SZ_K, (NB + 1) * 130).rearrange("p (b g e) -> p b g e", b=NB + 1, g=2)
    ring = [v3(bf(o_W + i * 2048, 1024), 8) for i in range(NSLOT)]
    xs = [f32(o_X + i * 4096, 1024) for i in range(2)]
    gt = [f32(o_X + 8192 + i * 4096, 1024) for i in range(2)]
    cf = f32(o_S, 576)
    cb = bf(o_S + 2304, 1152)
    small = nc.alloc_sbuf_tensor("small", [128, 192], F32)[:, :]
    c_bin = cf[:, 0:43]
    c_cw = cf[:, 64:64 + 8 * 31].rearrange("p (c k) -> p c k", c=8)
    c_cb = cf[:, 320:328]
    c_lg = cf[:, 328:336]
    c_lb = cf[:, 336:344]
    c_bcp = cf[:, 344:352]
    c_fw = cf[:, 352:352 + 132].rearrange("p (c k) -> p c k", c=44)
    c_fb = cf[:, 484:528]
    c_sink = cf[:, 528:544]
    ident = cb[:, 0:128]
    ones = cb[:, 128:256]
    m_same = cb[:, 256:384]
    m_prev = cb[:, 384:512]
    m1_same = cb[:, 512:640]
    m1_prev = cb[:, 640:768]
    m2_prev = cb[:, 768:896]
    m1_meta = cb[:, 896:1024]
    padmask = cb[:, 1024:1152]

    pst = nc.alloc_psum_tensor("ps", [128, 4096], F32)
    psb = [pst[:, i * 512:(i + 1) * 512] for i in range(8)]
    ps_i = {"all": 0, "att": 0, "ffn": 0}

    def psum2():
        i = ps_i["all"]
        if i % 2:
            i = (i + 1) % 8
        ps_i["all"] = (i + 2) % 8
        return pst[:, i * 512:(i + 2) * 512]

    def psum(pool="all"):
        if pool == "att":
            i = 3 + ps_i["att"]
            ps_i["att"] = (ps_i["att"] + 1) % 3
            return psb[i]
        if pool == "ln":
            return psum("att")
        if pool == "ffn":
            i = ps_i["ffn"]
            ps_i["ffn"] = (i + 1) % 7
            return psb[i]
        i = ps_i["all"]
        ps_i["all"] = (i + 1) % 8
        return psb[i]

    def dma(q, out, in_, stream, **kw):
        eng = {"sp": "sp", "pool": "pool", "act": "act"}[q]
        return P.add(eng, lambda e, o=out, i=in_: e.dma_start(out=o, in_=i), reads=[in_], writes=[out], dma=stream, **kw)

    def act(out, in_, func, bias=0.0, scale=1.0, accum=None, reads=(), big=None):
        r = [in_] + [a for a in (bias, scale) if not isinstance(a, float)] + list(reads)
        w = [out] + ([accum] if accum is not None else [])
        if big is None:
            big = out.free_size() >= 256 and accum is None

        def fn(e):
            kw = {}
            if accum is not None:
                kw["accum_out"] = accum
            return e.activation(out=out, in_=in_, func=func, bias=bias, scale=scale, **kw)
        return P.add("act", fn, reads=r, writes=w, big=big)

    def tt(eng, out, in0, in1, op):
        return P.add(eng, lambda e: e.tensor_tensor(out=out, in0=in0, in1=in1, op=op),
                     reads=[in0, in1], writes=[out], big=out.free_size() >= 256)

    def ts(eng, out, in0, s1, s2, op0, op1=None):
        r = [in0] + [a for a in (s1, s2) if a is not None and not isinstance(a, float)]

        def fn(e):
            if op1 is None:
                return e.tensor_scalar(out=out, in0=in0, scalar1=s1, scalar2=None, op0=op0)
            return e.tensor_scalar(out=out, in0=in0, scalar1=s1, scalar2=s2, op0=op0, op1=op1)
        return P.add(eng, fn, reads=r, writes=[out], big=out.free_size() >= 256)

    def stt(eng, out, in0, s, in1, op0, op1):
        r = [in0, in1] + ([] if isinstance(s, float) else [s])
        return P.add(eng, lambda e: e.scalar_tensor_tensor(out=out, in0=in0, scalar=s, in1=in1, op0=op0, op1=op1),
                     reads=r, writes=[out], big=out.free_size() >= 256)

    def recip(out, in_):
        return P.add("dve", lambda e: e.reciprocal(out=out, in_=in_), reads=[in_], writes=[out], big=out.free_size() >= 256)

    def mm_group(out, pairs):
        r = [a for p in pairs for a in p]

        def fn(e):
            n = len(pairs)
            ins = None
            for i, (l, rr) in enumerate(pairs):
                ins = e.matmul(out, lhsT=l, rhs=rr, start=(i == 0), stop=(i == n - 1))
            return ins
        return P.add("pe", fn, reads=r, writes=[out])

    def transposes(out_ps_bf, in_sb, n, rows=128):
        def fn(e):
            ins = None
            for i in range(n):
                ins = e.transpose(out=out_ps_bf[:, i * 128: i * 128 + rows],
                                  in_=in_sb[0:rows, i * 128:(i + 1) * 128], identity=ident[0:rows, 0:rows])
            return ins
        return P.add("pe", fn, reads=[in_sb[0:rows, :], ident], writes=[out_ps_bf[:, 0:n * 128]])

    def skewed(items, stages, rev=True):
        ns = len(stages)
        for step in range(len(items) + ns - 1):
            for s in (range(ns - 1, -1, -1) if rev else range(ns)):
                i = step - s
                if 0 <= i < len(items):
                    stages[s](items[i])

    epsb = small[:, 190:191]
    epsl = small[:, 191:192]

    def rms_rstd(s, r):
        act(s, s, AF.Sqrt, bias=epsb[0:s.shape[0], :], scale=1.0 / D)
        recip(r, s)

    slot_i = [0]

    def load_chunk(src):
        s = slot_i[0]
        slot_i[0] = (s + 1) % NSLOT
        dma("pool", ring[s], src, "w%d" % s)
        return ring[s]

    dma("sp", cf, cst_f, "const")
    dma("pool", cb, cst_b, "constb")
    dma("sp", gt[0], g4[0], "g0")
    esink = small[:, 0:16]
    act(esink, c_sink, AF.Exp)
    P.add("dve", lambda e: e.memset(epsb, 1e-6), writes=[epsb])
    P.add("dve", lambda e: e.memset(epsl, 1e-5), writes=[epsl])
    P.add("dve", lambda e: e.memset(vA[:, :, :, 64:65], 1.0), writes=[vA[:, :, :, 64:65]])
    TB = o_T
    bvt = f32(TB, 128)
    TB2 = TB + 512
    dma("sp", bvt, bv_bc, "const2")

    sqj = bf(TB2, 1024)
    xh = [bf(TB2 + 2048 + i * 2048, 1024) for i in range(2)]
    ss = small[:, 16:48]

    xs1 = [xs[0], xs[1], gt[1], f32(TB2 + 8192, 1024)]

    def p1_a(b):
        nrows = 128 if b < NB else NMETA
        xt = xs1[b % 4]
        dma("sp", xt[0:nrows, :], xin[b * 128:b * 128 + nrows, :], "xp%d" % (b % 4))
        s = ss[0:nrows, (b % 16) * 2:(b % 16) * 2 + 1]
        r = ss[0:nrows, (b % 16) * 2 + 1:(b % 16) * 2 + 2]
        act(sqj[0:nrows, :], xt[0:nrows, :], AF.Square, accum=s)
        rms_rstd(s, r)

    p1s = {}

    def p1_b(b):
        nrows = 128 if b < NB else NMETA
        xt = xs1[b % 4]
        r = ss[0:nrows, (b % 16) * 2 + 1:(b % 16) * 2 + 2]
        xo = xh[b % 2]
        stt("dve", xo[0:nrows, :], xt[0:nrows, :], r, gt[0][0:nrows, :], ALU.mult, ALU.mult)
        pt = psum()
        ptb = pt[:, :].bitcast(BF16)
        transposes(ptb, xo, 8, rows=nrows)
        p1s[b] = ptb

    def p1_c(b):
        nrows = 128 if b < NB else NMETA
        ptb = p1s.pop(b)
        src = ptb.rearrange("p (c t) -> p c t", c=8)[:, :, 0:nrows]
        dst = xT[:, :, b * 128:b * 128 + nrows]
        P.add("dve", lambda e, dst=dst, src=src: e.tensor_copy(out=dst, in_=src), reads=[src], writes=[dst], big=nrows >= 32)

    CH_Q, CH_K, CH_V, CH_A, CH_G, CH_GA, CH_GC = 0, 8, 10, 11, 19, 27, 35
    sg = [f32(TB2 + 16384 + i * 2048, 512) for i in range(2)]

    def proj(wt, t0, t1, pool="all"):
        pt = psum(pool)
        o = pt[:, 0:t1 - t0]
        mm_group(o, [(wt[:, kc, :], xT[:, kc, t0:t1]) for kc in range(8)])
        return o

    assert 3 * NTM * 2 <= 4 * 4096
    tabC = bf(o_X, NTM)
    tabS = bf(o_X + NTM * 2, NTM)
    Pt = bf(o_X + NTM * 4, NTM)
    rtmp = f32(TB2 + 8192, 512)

    def rope_work(X, lo, hi):
        n = hi - lo
        out = []

        def swaps():
            for (a, b_) in ((0, 8), (8, 0), (64, 72), (72, 64)):
                dma("sp", Pt[a:a + 8, lo:hi], X[b_:b_ + 8, 0:n], "rp")
        out.append(swaps)
        for (t0, t1) in tiles(lo, hi):
            def grp(t0=t0, t1=t1):
                tm = rtmp[:, 0:t1 - t0]
                xsl = X[:, t0 - lo:t1 - lo]
                tt("dve", tm, Pt[:, t0:t1], tabS[:, t0:t1], ALU.mult)
                tt("dve", xsl, xsl, tabC[:, t0:t1], ALU.mult)
                tt("dve", xsl, xsl, tm, ALU.add)
            out.append(grp)
        return out

    def k_job(g, wk, t0, t1):
        pk = proj(wk, t0, t1)
        act(kT[:, g, t0:t1], pk, AF.Identity, bias=c_bin[:, CH_K + g:CH_K + g + 1])

    def q_job(c, wq, t0, t1):
        pq = proj(wq, t0, t1)
        act(Bq[:, c, t0 - M0:t1 - M0], pq, AF.Identity, bias=c_bin[:, CH_Q + c:CH_Q + c + 1])

    def v_job(b):
        nrows = 128 if b < NB else NMETA
        pt = psum()
        o = pt[0:nrows, 0:128]
        mm_group(o, [(xT[:, kc, b * 128:b * 128 + nrows], wv[:, kc, :]) for kc in range(8)])
        tt("dve", vA[0:nrows, b, :, 0:64], o.rearrange("p (g e) -> p g e", g=2),
           bvt[0:nrows, :].rearrange("p (g e) -> p g e", g=2), ALU.add)

    NQ0 = 3
    wks = [load_chunk(win_l[CH_K + g]) for g in range(2)]
    wv = load_chunk(win_l[CH_V])
    wq0 = [load_chunk(win_l[CH_Q + c]) for c in range(NQ0)]
    jobs = []
    for g in range(2):
        for (t0, t1) in tiles(0, NTM):
            need = NB if t1 > NT else (t1 - 1) // 128
            jobs.append((need, lambda g=g, t0=t0, t1=t1: k_job(g, wks[g], t0, t1)))
    for c in range(NQ0):
        for (t0, t1) in tiles(M0, NT):
            jobs.append(((t1 - 1) // 128, lambda c=c, t0=t0, t1=t1: q_job(c, wq0[c], t0, t1)))
    jobs.sort(key=lambda j: j[0])
    for step in range(NB + 3):
        if 0 <= step - 2 <= NB:
            p1_c(step - 2)
        if 0 <= step - 1 <= NB:
            p1_b(step - 1)
        if step <= NB:
            p1_a(step)
        done = step - 2
        if 0 <= done <= NB:
            v_job(done)
            while jobs and jobs[0][0] <= done:
                jobs.pop(0)[1]()
    assert not jobs
    dma("pool", tabC, ropeC, "rtc")
    dma("pool", tabS, ropeS, "rts")
    P.add("dve", lambda e: e.memset(Pt, 0.0), writes=[Pt])
    for c in range(NQ0, 8):
        wq = load_chunk(win_l[CH_Q + c])
        for (t0, t1) in tiles(M0, NT):
            q_job(c, wq, t0, t1)
    rope_list = []
    for (X, lo, hi) in [(kT[:, g, 0:NTM], 0, NTM) for g in range(2)] + [(Bq[:, c, :], M0, NT) for c in range(8)]:
        rope_list += rope_work(X, lo, hi)
    rope_rate = len(rope_list) / float(8 * len(tiles(M0, NT)) - 2)
    rope_credit = [0.0]

    for c in range(8):
        wa = load_chunk(win_l[CH_A + c])
        wg = load_chunk(win_l[CH_G + c])
        for i, (t0, t1) in enumerate(tiles(M0, NT)):
            pa = proj(wa, t0, t1)
            pg = proj(wg, t0, t1)
            s_ = sg[i % 2][:, 0:t1 - t0]
            act(s_, pg, AF.Sigmoid, bias=c_bin[:, CH_G + c:CH_G + c + 1])
            stt("dve", Cg[:, c, t0 - M0:t1 - M0], pa, c_bin[:, CH_A + c:CH_A + c + 1], s_, ALU.add, ALU.mult)
            rope_credit[0] += rope_rate
            while rope_credit[0] >= 1.0 - 1e-9 and rope_list:
                rope_list.pop(0)()
                rope_credit[0] -= 1.0
        tt("dve", Cg[:, c, 0:128], Cg[:, c, 0:128], padmask, ALU.mult)
    while rope_list:
        rope_list.pop(0)()

    dgbs = [bf(o_X + i * 8192, 31 * 128).rearrange("p (k m) -> p k m", k=31) for i in range(2)]
    P.add("dve", lambda e: e.memset(Dc[:, :, 0:32], 0.0), writes=[Dc[:, :, 0:32]])
    CT = tiles(M0 + 32, NT)

    def conv_tile(c, ti):
        dgb = dgbs[c % 2]
        if ti == 0:
            idb = bass.AP(tensor=ident.tensor, offset=ident.offset, ap=[list(ident.ap[0]), [0, 31], [1, 128]])
            wsl = c_cw[:, c, :]
            wb = bass.AP(tensor=wsl.tensor, offset=wsl.offset, ap=[list(wsl.ap[0]), [1, 31], [0, 128]])
            P.add("dve", lambda e: e.tensor_tensor(out=dgb, in0=idb, in1=wb, op=ALU.mult),
                  reads=[ident, wsl], writes=[dgb], big=True)
        t0, t1 = CT[ti]
        n = t1 - t0
        pc = psb[6][:, 0:n]
        mm_group(pc, [(dgb[:, k, :], Cg[:, c, t0 - M0 - 30 + k:t1 - M0 - 30 + k]) for k in range(31)])
        act(Dc[:, c, t0 - M0:t1 - M0], pc, AF.Identity, bias=c_cb[:, c:c + 1])

    conv_items = [(c, ti) for c in range(8) for ti in range(len(CT))]

    PTW = 12
    pts = [bf(TB2 + i * 1024, 512) for i in range(PTW)]
    osbs = [bf(TB2 + 12288 + i * 2048, 1024) for i in range(2)]
    dens = [small[:, 48 + 16 * i:64 + 16 * i] for i in range(2)]
    rdens = [small[:, 80 + 16 * i:96 + 16 * i] for i in range(2)]
    pti = [0]
    HB = [(0, 7), (7, 14), (14, 16)]

    def o_region(b, h):
        for bi, (h0, h1) in enumerate(HB):
            if h0 <= h < h1:
                return psb[bi][:, (h - h0) * 65:(h - h0) * 65 + 65]

    att_jobs = {}

    def att_a(u):
        b, g = u
        q0 = b * 128 - M0
        jobs = {h: [] for h in range(8 * g, 8 * g + 8)}
        for kc in range(3):
            if kc == 0:
                nk, ktok, vblk = NMETA, NT, NB
                msk = m1_meta if b == 1 else None
            elif kc == 1:
                nk, ktok, vblk = 128, (b - 1) * 128, b - 1
                msk = m1_prev if b == 1 else (m2_prev if b == 2 else m_prev)
            else:
                nk, ktok, vblk = 128, b * 128, b
                msk = m1_same if b == 1 else m_same
            for par in range(2):
                p0 = par * 64
                pt = psum("att")
                o = pt[0:nk, :]
                rhs = Bq[p0:p0 + 64, 4 * g:4 * g + 4, q0:q0 + 128]
                lhsT = kT[p0:p0 + 64, g, ktok:ktok + nk]
                P.add("pe", lambda e, o=o, l=lhsT, r=rhs: e.matmul(o, lhsT=l, rhs=r, start=True, stop=True),
                      reads=[lhsT, rhs], writes=[o])
                pT = pts[pti[0] % PTW]
                pti[0] += 1
                act(pT[0:nk, :], o, AF.Exp, scale=HD ** -0.5)
                if msk is not None:
                    pv = pT[0:nk, :].rearrange("p (j q) -> p j q", j=4)
                    mk = msk[0:nk, :]
                    mb = bass.AP(tensor=mk.tensor, offset=mk.offset, ap=[list(mk.ap[0]), [0, 4], [1, 128]])
                    P.add("dve", lambda e, pv=pv, mb=mb: e.tensor_tensor(out=pv, in0=pv, in1=mb, op=ALU.mult),
                          reads=[pv, mk], writes=[pv], big=True)
                for j in range(4):
                    h = 8 * g + 2 * j + par
                    jobs[h].append((pT[0:nk, j * 128:(j + 1) * 128], vA[0:nk, vblk, g, :]))
        att_jobs[u] = jobs

    def att_b(u):
        b, g = u
        jobs = att_jobs.pop(u)
        for h in range(8 * g, 8 * g + 8):
            mm_group(o_region(b, h), jobs[h])

    def att_f(u):
        b, g = u
        if g != 1:
            return
        q0 = b * 128 - M0
        den, rden, osb = dens[b % 2], rdens[b % 2], osbs[b % 2]
        for bi, (h0, h1) in enumerate(HB):
            nh = h1 - h0
            ov = psb[bi][:, 0:nh * 65].rearrange("p (h e) -> p h e", h=nh)
            tt("dve", den[:, h0:h1], ov[:, :, 64], esink[:, h0:h1], ALU.add)
        recip(rden, den)
        for bi, (h0, h1) in enumerate(HB):
            nh = h1 - h0
            ov = psb[bi][:, 0:nh * 65].rearrange("p (h e) -> p h e", h=nh)
            rd = rden[:, h0:h1]
            rb = bass.AP(tensor=rd.tensor, offset=rd.offset, ap=[list(rd.ap[0]), [1, nh], [0, 64]])
            ob = osb[:, h0 * 64:h1 * 64].rearrange("p (h e) -> p h e", h=nh)
            P.add("dve", lambda e, ob=ob, ov=ov, rb=rb: e.tensor_tensor(out=ob, in0=ov[:, :, 0:64], in1=rb, op=ALU.mult),
                  reads=[ov, rd], writes=[ob], big=True)
        ptb = psb[7].bitcast(BF16)
        transposes(ptb, osb, 8)
        act(Bq[:, :, q0:q0 + 128], ptb.rearrange("p (c t) -> p c t", c=8), AF.Copy)

    LW = 256
    sq = [bf(o_X + i * 512, LW) for i in range(2)]
    mean = f32(o_X + 1024, LW)
    rstd = f32(o_X + 2048, LW)
    msq = f32(o_X + 3072, LW)
    tln = [f32(o_X + 4096 + i * 1024, LW) for i in range(2)]

    def ln_item(t0, t1):
        n = t1 - t0
        p1 = psum("ln")[:, 0:n]
        mm_group(p1, [(ones, Dc[:, c, t0 - M0:t1 - M0]) for c in range(8)])
        p2 = psum("ln")[:, 0:n]
        for c in range(8):
            tt("dve", sq[c % 2][:, 0:n], Dc[:, c, t0 - M0:t1 - M0], Dc[:, c, t0 - M0:t1 - M0], ALU.mult)
            P.add("pe", lambda e, p2=p2, c=c, s_=sq[c % 2][:, 0:n]: e.matmul(p2, lhsT=ones, rhs=s_, start=(c == 0), stop=(c == 7)),
                  reads=[ones, sq[c % 2][:, 0:n]] + ([p2] if c else []), writes=[p2])
        ts("dve", mean[:, 0:n], p1, 1.0 / D, None, ALU.mult)
        tt("dve", msq[:, 0:n], mean[:, 0:n], mean[:, 0:n], ALU.mult)
        stt("dve", rstd[:, 0:n], p2, 1.0 / D, msq[:, 0:n], ALU.mult, ALU.subtract)
        act(rstd[:, 0:n], rstd[:, 0:n], AF.Sqrt, bias=epsl, big=True)
        recip(rstd[:, 0:n], rstd[:, 0:n])
        for c in range(8):
            t_ = tln[c % 2][:, 0:n]
            dsl = Dc[:, c, t0 - M0:t1 - M0]
            tt("dve", t_, dsl, mean[:, 0:n], ALU.subtract)
            tt("dve", dsl, t_, rstd[:, 0:n], ALU.mult)

    def silu_cols(t0, t1):
        for c in range(8):
            dsl = Dc[:, c, t0 - M0:t1 - M0]
            act(dsl, dsl, AF.Silu, bias=c_lb[:, c:c + 1], scale=c_lg[:, c:c + 1], big=True)

    ln_items = tiles(M0, NT, LW)
    ln_done = []
    units = [(b, g) for b in range(1, NB) for g in range(2)]
    rate = len(conv_items) / (0.62 * len(units))
    credit = 0.0
    for k in range(len(units) + 1):
        if k < len(units):
            att_a(units[k])
        if k >= 1:
            u = units[k - 1]
            att_b(u)
            credit += rate
            while credit >= 1.0 - 1e-9 and conv_items:
                conv_tile(*conv_items.pop(0))
                credit -= 1.0
            att_f(u)
            if not conv_items and ln_items:
                lt = ln_items.pop(0)
                ln_item(*lt)
                ln_done.append(lt)
    while conv_items:
        conv_tile(*conv_items.pop(0))
    if ln_done:
        for c in range(8):
            for (t0, t1) in tiles(ln_done[0][0], ln_done[-1][1]):
                dsl = Dc[:, c, t0 - M0:t1 - M0]
                act(dsl, dsl, AF.Silu, bias=c_lb[:, c:c + 1], scale=c_lg[:, c:c + 1])

    sga = [f32(TB2 + i * 2048, 512) for i in range(2)]
    sgc = [f32(TB2 + 4096 + i * 2048, 512) for i in range(2)]
    t1b = [f32(TB2 + 8192 + i * 2048, 512) for i in range(2)]
    t2b = [f32(TB2 + 12288 + i * 2048, 512) for i in range(2)]
    for c in range(8):
        w_ap = load_chunk(wap_l[c])
        w_cp = load_chunk(wcp_l[c])
        w_ga = load_chunk(win_l[CH_GA + c])
        w_gc = load_chunk(win_l[CH_GC + c])
        for i, (t0, t1) in enumerate(tiles(M0, NT)):
            n = t1 - t0
            a0, a1 = t0 - M0, t1 - M0
            if c == 0:
                while ln_items and ln_items[0][0] < t1:
                    lt = ln_items.pop(0)
                    ln_item(*lt)
                    silu_cols(*lt)
            pga = proj(w_ga, t0, t1)
            pgc = proj(w_gc, t0, t1)
            pat = psum()[:, 0:n]
            mm_group(pat, [(w_ap[:, kc, :], Bq[:, kc, a0:a1]) for kc in range(8)])
            pcv = psum()[:, 0:n]
            mm_group(pcv, [(w_cp[:, kc, :], Dc[:, kc, a0:a1]) for kc in range(8)])
            s1 = sga[i % 2][:, 0:n]
            s2 = sgc[i % 2][:, 0:n]
            act(s1, pga, AF.Sigmoid, bias=c_bin[:, CH_GA + c:CH_GA + c + 1])
            act(s2, pgc, AF.Sigmoid, bias=c_bin[:, CH_GC + c:CH_GC + c + 1])
            u1 = t1b[i % 2][:, 0:n]
            u2 = t2b[i % 2][:, 0:n]
            tt("dve", u1, pat, s1, ALU.mult)
            stt("dve", u2, pcv, c_bcp[:, c:c + 1], s2, ALU.add, ALU.mult)
            tt("dve", Cg[:, c, a0:a1], u1, u2, ALU.add)

    if SMALL:
        o_WD = o_E
        o_WO = o_E + NF * D * 2
        o_H = o_WO + 8 * D * 2
    else:
        o_WD = o_W - NF * D * 2
        o_WO = o_WD - 8 * D * 2
        o_H = o_C
        assert o_WO >= o_B
    wo = bf(o_WO, 8 * D).rearrange("p (k n) -> p k n", k=8)
    dma("pool", wo, wout_l, "wo")
    dma("sp", gt[0], g4[1], "g0")
    dma("sp", gt[1], g4[2], "g1")
    wd = bf(o_WD, NF * D).rearrange("p (f n) -> p f n", f=NF)
    yt = [f32(TB2 + i * 4096, 1024) for i in range(2)]
    sqj7 = bf(TB2 + 8192, 1024)
    xh7 = [bf(TB2 + 10240 + i * 2048, 1024) for i in range(2)]
    ss7 = small[:, 112:144]
    p7 = {}

    p7pair = [0]
    p7tb = [0]

    def p7_a(b):
        a0 = b * 128 - M0
        i = p7pair[0]
        p7pair[0] = (i + 1) % 3
        pAB = pst[:, i * 1024:(i + 1) * 1024]
        p7[b] = pAB
        for half in range(2):
            mm_group(pAB[:, half * 512:(half + 1) * 512],
                     [(Cg[:, kc, a0:a0 + 128], wo[:, kc, half * 512:(half + 1) * 512]) for kc in range(8)])
        xt = xs[b % 2]
        dma("sp", xt, xin[b * 128:(b + 1) * 128, :], "x%d" % (b % 2))

    def p7_b(b):
        j = b % 8
        s = ss7[:, 4 * j:4 * j + 1]
        act(sqj7, p7[b], AF.Square, accum=s)
        act(s, s, AF.Sqrt, bias=epsb, scale=1.0 / D)

    def p7_c(b):
        pAB = p7.pop(b)
        xt = xs[b % 2]
        j = b % 8
        s, r = ss7[:, 4 * j:4 * j + 1], ss7[:, 4 * j + 1:4 * j + 2]
        h1 = yt[b % 2]
        recip(r, s)
        stt("dve", h1, pAB, r, gt[0], ALU.mult, ALU.mult)
        tt("dve", h1, h1, xt, ALU.add)
        if b >= 2:
            dma("pool", h1s[(b - 2) * 128:(b - 1) * 128, :], h1, "h%d" % (b % 2), extra_w=[("h1s", b)])

    def p7_d(b):
        j = b % 8
        s2 = ss7[:, 4 * j + 2:4 * j + 3]
        act(sqj7, yt[b % 2], AF.Square, accum=s2)
        act(s2, s2, AF.Sqrt, bias=epsb, scale=1.0 / D)

    def p7_e(b):
        j = b % 8
        s2, r2 = ss7[:, 4 * j + 2:4 * j + 3], ss7[:, 4 * j + 3:4 * j + 4]
        recip(r2, s2)
        stt("dve", xh7[b % 2], yt[b % 2], r2, gt[1], ALU.mult, ALU.mult)

    def p7_f(b):
        i = p7tb[0]
        p7tb[0] = (i + 1) % 2
        ptb = psb[6 + i].bitcast(BF16)
        transposes(ptb, xh7[b % 2], 8)
        p7[("t", b)] = ptb

    def p7_g(b):
        ptb = p7.pop(("t", b))
        act(xT[:, :, b * 128:(b + 1) * 128], ptb.rearrange("p (c t) -> p c t", c=8), AF.Copy)

    skewed(list(range(1, NB)), [p7_a, p7_b, p7_c, p7_d, p7_e, p7_f, p7_g])

    SZ_H = NF * HTOK * 2
    hT = bf(o_H, NF * HTOK).rearrange("p (f t) -> p f t", f=NF)
    assert SMALL or o_C + SZ_H <= o_WD, (SZ_H, o_WD)
    dma("sp", gt[0], g4[3], "g0")
    NAC = 3
    ac = [[f32(TB2 + (2 * s_ + k) * 2048, 512) for k in range(2)] for s_ in range(NAC)]
    cys = [[small[:, 164 + 4 * s_ + 2 * k:166 + 4 * s_ + 2 * k] for k in range(2)] for s_ in range(NAC)]
    cy0 = [small[:, 176 + 2 * i:178 + 2 * i] for i in range(4)]
    ot = [f32(TB2 + 12288 + i * 4096, 1024) for i in range(2)]
    sqj9 = bf(o_X + 12288, 1024)
    ss9 = small[:, 148:164]
    minib = psb[7]
    mini_i = [0]
    ffn_state = {}

    for hf in range(NHALF):
        T0 = OWN0 + hf * HTOK
        TL = tiles(T0, T0 + HTOK)
        units = [(f, ti) for f in range(NF) for ti in range(len(TL))]
        wts = {}

        def f_a(u, T0=T0, TL=TL, hf=hf):
            f, ti = u
            t0, t1 = TL[ti]
            n = t1 - t0
            if ti == 0:
                wts[f] = (load_chunk(wup_l[f]), load_chunk(wup_l[NF + f]))
                if hf == 0:
                    dma("pool", wd[:, f, :], wdn_l[:, f, :], "wd")
            ui = f * len(TL) + ti
            sl = ui % NAC
            st = {}
            for k, ch in enumerate((f, NF + f)):
                wt = wts[f][k]
                if ti == 0:
                    mi = mini_i[0]
                    mini_i[0] += 1
                    cyp = minib[:, 2 * (mi % 256):2 * (mi % 256) + 2]
                    mm_group(cyp, [(wt[:, kc, :], xT[:, kc, t0 - 2:t0]) for kc in range(8)])
                    cy = cy0[mi % 4]
                    act(cy, cyp, AF.Copy, big=False)
                else:
                    cy = ffn_state[(f, ti - 1)]["cy"][k]
                pu = proj(wt, t0, t1, "ffn")
                a_ = ac[sl][k][:, 0:n]
                act(a_, pu, AF.Identity, bias=c_fb[:, ch:ch + 1], scale=c_fw[:, ch, 2:3])
                if ti + 1 < len(TL):
                    cn = cys[sl][k]
                    act(cn, pu[:, n - 2:n], AF.Copy, big=False)
                    st.setdefault("cy", {})[k] = cn
                w1, w0 = c_fw[:, ch, 1:2], c_fw[:, ch, 0:1]
                stt("dve", a_[:, 1:n], pu[:, 0:n - 1], w1, a_[:, 1:n], ALU.mult, ALU.add)
                stt("dve", a_[:, 2:n], pu[:, 0:n - 2], w0, a_[:, 2:n], ALU.mult, ALU.add)
                stt("dve", a_[:, 0:1], cy[:, 1:2], w1, a_[:, 0:1], ALU.mult, ALU.add)
                stt("dve", a_[:, 0:2], cy[:, 0:2], w0, a_[:, 0:2], ALU.mult, ALU.add)
            ffn_state[(f, ti)] = st

        def f_b(u, T0=T0, TL=TL):
            f, ti = u
            t0, t1 = TL[ti]
            n = t1 - t0
            ui = f * len(TL) + ti
            sl = ui % NAC
            ag = ac[sl][0][:, 0:n]
            av = ac[sl][1][:, 0:n]
            act(ag, ag, AF.Silu)
            tt("dve", hT[:, f, t0 - T0:t1 - T0], ag, av, ALU.mult)

        skewed(units, [f_a, f_b], rev=False)

        p9 = {}

        def p9_a(bb, hf=hf):
            blk = hf * (HTOK // 128) + bb
            pAB = psum2()
            p9[bb] = pAB
            for half in range(2):
                mm_group(pAB[:, half * 512:(half + 1) * 512],
                         [(hT[:, f, bb * 128:(bb + 1) * 128], wd[:, f, half * 512:(half + 1) * 512]) for f in range(NF)])
            ht = xs[blk % 2]
            dma("sp", ht, h1s[blk * 128:(blk + 1) * 128, :], "x%d" % (blk % 2), extra_r=[("h1s", blk + 2)])
            s = ss9[:, 2 * (blk % 8):2 * (blk % 8) + 1]
            r = ss9[:, 2 * (blk % 8) + 1:2 * (blk % 8) + 2]
            act(sqj9, pAB, AF.Square, accum=s)
            rms_rstd(s, r)

        def p9_b(bb, hf=hf):
            blk = hf * (HTOK // 128) + bb
            pAB = p9.pop(bb)
            ht = xs[blk % 2]
            r = ss9[:, 2 * (blk % 8) + 1:2 * (blk % 8) + 2]
            o_ = ot[blk % 2]
            stt("dve", o_, pAB, r, gt[0], ALU.mult, ALU.mult)
            tt("dve", o_, o_, ht, ALU.add)
            dma("pool", y[blk * 128:(blk + 1) * 128, :], o_, "y%d" % (blk % 2))

        skewed(list(range(HTOK // 128)), [p9_a, p9_b])

    P.finalize_and_emit(final_streams=["y0", "y1"])
    return nc, P


def _chunk(w, cols):
    sub = w[:, cols]
    return sub.reshape(8, 128, 128).transpose(1, 0, 2)


def host_prep(inp, NOWN=16, n_cores=8):
    f = np.float32
    x = np.asarray(inp["x"], f)
    B, S, _ = x.shape
    assert S == 2 * NOWN * 128 and B * 2 == n_cores
    NB = NOWN + 2
    NT = NB * 128
    NTM = NT + NMETA
    meta = np.asarray(inp["meta_tokens"], f)
    w_in = np.asarray(inp["w_in"], f)[0]
    b_in = np.asarray(inp["b_in"], f)[0]
    ar = np.arange(128)
    cols = [c * 128 + ar for c in range(8)]
    cols += [1024 + 64 * g + (ar % 64) for g in range(2)]
    cols += [1152 + ar]
    cols += [1280 + c * 128 + ar for c in range(8)]
    cols += [2304 + c * 128 + ar for c in range(8)]
    cols += [3328 + c * 128 + ar for c in range(8)]
    cols += [4352 + c * 128 + ar for c in range(8)]
    win_l = np.ascontiguousarray(np.stack([_chunk(w_in, c) for c in cols]))
    bin_l = np.stack([b_in[c] for c in cols], axis=1)
    wap = np.asarray(inp["w_attn_proj"], f)[0]
    wcp = np.asarray(inp["w_conv_proj"], f)[0]
    wap_l = np.ascontiguousarray(np.stack([_chunk(wap, c * 128 + ar) for c in range(8)]))
    wcp_l = np.ascontiguousarray(np.stack([_chunk(wcp, c * 128 + ar) for c in range(8)]))
    wout_l = np.ascontiguousarray(np.asarray(inp["w_out"], f)[0].reshape(8, 128, D).transpose(1, 0, 2))
    w_up = np.asarray(inp["w_up"], f)[0]
    wup_l = np.ascontiguousarray(np.stack([_chunk(w_up, c * 128 + ar) for c in range(44)]))
    wdn_l = np.ascontiguousarray(np.asarray(inp["w_down"], f)[0].reshape(NF, 128, D).transpose(1, 0, 2))

    def fm(v):
        return np.asarray(v, f).reshape(-1, 128).T

    cst_f = np.zeros((128, 576), f)
    cst_f[:, 0:43] = bin_l
    cw = np.asarray(inp["conv_dw_w"], f)[0]
    cst_f[:, 64:64 + 248] = cw.reshape(31, 8, 128).transpose(2, 1, 0).reshape(128, 248)
    cst_f[:, 320:328] = fm(inp["conv_dw_b"][0])
    cst_f[:, 328:336] = fm(inp["conv_ln_g"][0])
    cst_f[:, 336:344] = fm(inp["conv_ln_b"][0])
    cst_f[:, 344:352] = fm(inp["b_conv_proj"][0])
    fw = np.asarray(inp["ffn_dw_w"], f)[0]
    cst_f[:, 352:352 + 132] = fw.reshape(3, 44, 128).transpose(2, 1, 0).reshape(128, 132)
    cst_f[:, 484:528] = fm(inp["ffn_dw_b"][0])
    cst_f[:, 528:544] = np.broadcast_to(np.asarray(inp["attn_sinks"], f)[0][None, :], (128, 16))
    g4 = np.stack([np.broadcast_to(np.asarray(inp[k], f)[0][None, :], (128, D))
                   for k in ("norm_pre_mix", "norm_post_mix", "norm_pre_ffn", "norm_post_ffn")])
    g4 = np.ascontiguousarray(g4)
    bv_bc = np.ascontiguousarray(np.broadcast_to(b_in[1152:1280][None, :], (128, 128)))

    kk = np.arange(128)[:, None]
    qq = np.arange(128)[None, :]
    same = (kk <= qq).astype(f)
    prev = (kk > qq).astype(f)
    inv_freq = (500000.0 ** (-np.arange(8, dtype=f) * f(2.0) / f(16))).astype(f)
    maps = []
    for c in range(n_cores):
        b, half = c // 2, c % 2
        xin = np.zeros((NTM, D), f)
        if half == 0:
            xin[240:256] = meta
            xin[256:NT] = x[b, 0:NOWN * 128]
        else:
            xin[0:NT] = x[b, NOWN * 128 - 256:2 * NOWN * 128]
        xin[NT:NTM] = meta
        pos = half * NOWN * 128 + np.arange(NT) - 240
        pos = np.maximum(pos, 0)
        pos = np.concatenate([pos, np.arange(NMETA)]).astype(f)
        ang = pos[None, :] * inv_freq[:, None]
        cosv, sinv = np.cos(ang).astype(f), np.sin(ang).astype(f)
        C = np.ones((128, NTM), f)
        S_ = np.zeros((128, NTM), f)
        for base in (0, 64):
            C[base:base + 8] = cosv
            C[base + 8:base + 16] = cosv
            S_[base:base + 8] = -sinv
            S_[base + 8:base + 16] = sinv
        cb = np.zeros((128, 1152), f)
        cb[:, 0:128] = np.eye(128, dtype=f)
        cb[:, 128:256] = 1.0
        cb[:, 256:384] = same
        cb[:, 384:512] = prev
        if half == 0:
            mm = np.ones((128, 128), f)
            m_idx = np.arange(128)[:, None]
            mq = (np.arange(128)[None, :] - 112)
            mm = np.where(mq >= 0, (m_idx <= mq), True).astype(f)
            cb[:, 896:1024] = mm
            pm = np.zeros((128, 128), f)
            pm[:, 112:128] = 1.0
            cb[:, 1024:1152] = pm
        else:
            cb[:, 512:640] = same
            cb[:, 640:768] = prev
            cb[:, 768:896] = prev
            cb[:, 896:1024] = 1.0
            cb[:, 1024:1152] = 1.0
        maps.append({
            "xin": xin, "win_l": win_l, "wap_l": wap_l, "wcp_l": wcp_l, "wout_l": wout_l,
            "wup_l": wup_l, "wdn_l": wdn_l, "ropeC": C, "ropeS": S_, "cst_f": cst_f,
            "cst_b": cb, "g4": g4, "bv_bc": bv_bc,
        })
    return maps


_CACHE = {}


def kernel(**inputs):
    NOWN = 16
    maps = host_prep(inputs, NOWN, 8)
    if "nc" not in _CACHE:
        _CACHE["nc"] = build(NOWN)[0]
    nc = _CACHE["nc"]
    res = run_bass_kernel_spmd(nc, maps, core_ids=list(range(8)))
    B = 4
    out = np.zeros((B, 2 * NOWN * 128, D), np.float32)
    for c in range(8):
        out[c // 2, (c % 2) * NOWN * 128:(c % 2 + 1) * NOWN * 128] = res.results[c]["y"]
    return out
```
